# Optimizing a Trainium2 kernel written in Bass

```python
import math
import jax, jax.numpy as jnp
from jax import lax
import numpy as np

D_MODEL = 1024
BATCH = 8
SEQ = 2048
DEPTH = 1
DEC_BATCH = 16
DEC_SEQ = 32
PAST_LEN = 2048

CHUNK = 64
D_MIX = D_MODEL
ATT_HEAD_DIM = 64
D_ATT = D_MIX // 2
N_ATT_HEADS = D_ATT // ATT_HEAD_DIM
D_SSM = D_MIX - D_ATT
SSM_HEAD_DIM = 64
N_SSM_HEADS = D_SSM // SSM_HEAD_DIM
SSM_GROUPS = 2
SSM_HEADS_PER_GROUP = N_SSM_HEADS // SSM_GROUPS
SSM_STATE = 64
SSM_CONV = 4
SSM_CHUNK = CHUNK
CONV_DIM = D_SSM + 2 * SSM_GROUPS * SSM_STATE
D_FF = 2816
FFN_CONV = 3
Q_BLOCK = 128
D_IN_PROJ = 3 * D_ATT + D_SSM + CONV_DIM + N_SSM_HEADS
NORM_EPS = 1e-6

kernel_name = 'hybrid_stickbreak_ssd_convffn_stream_step'


def rmsnorm(x, g):
    x32 = x.astype(jnp.float32)
    y = x32 * lax.rsqrt(jnp.mean(x32 * x32, axis=-1, keepdims=True) + NORM_EPS)
    return (y * g.astype(jnp.float32)).astype(x.dtype)


def causal_dwconv(x, prev, w, bias):
    K = w.shape[0]
    l = x.shape[1]
    xp = jnp.concatenate([prev.astype(x.dtype), x], axis=1)
    out = bias.astype(x.dtype)
    for i in range(K):
        out = out + xp[:, i:i + l] * w[i].astype(x.dtype)
    return out, xp[:, l:]


def _stick_breaking_block(q, q_pos, k, v, k_pos):
    z = jnp.einsum('bqhd,bkhd->bhqk', q, k).astype(jnp.float32) * (ATT_HEAD_DIM ** -0.5)
    causal = k_pos[None, :] < q_pos[:, None]
    log_keep = jnp.where(causal, jax.nn.log_sigmoid(-z), 0.0)
    log_after = lax.cumsum(log_keep, axis=3, reverse=True) - log_keep
    w = jnp.where(causal, jnp.exp(jax.nn.log_sigmoid(z) + log_after), 0.0)
    return jnp.einsum('bhqk,bkhd->bqhd', w.astype(v.dtype), v)


def stick_breaking(q, q_pos, k, v, k_pos):
    b, l, h, d = q.shape
    if l > Q_BLOCK and l % Q_BLOCK == 0:
        nb = l // Q_BLOCK
        qb = q.reshape(b, nb, Q_BLOCK, h, d).transpose(1, 0, 2, 3, 4)
        pb = q_pos.reshape(nb, Q_BLOCK)
        ob = lax.map(lambda a: _stick_breaking_block(a[0], a[1], k, v, k_pos), (qb, pb))
        return ob.transpose(1, 0, 2, 3, 4).reshape(b, l, h, d)
    return _stick_breaking_block(q, q_pos, k, v, k_pos)


def ssd_scan(x, dt, a, bmat, cmat, h0):
    bsz, l = x.shape[0], x.shape[1]
    q = min(SSM_CHUNK, l)
    c = l // q
    G, R, P, N = SSM_GROUPS, SSM_HEADS_PER_GROUP, SSM_HEAD_DIM, SSM_STATE
    xs = x.reshape(bsz, c, q, G, R, P)
    dts = dt.reshape(bsz, c, q, G, R)
    bs = bmat.reshape(bsz, c, q, G, N)
    cs = cmat.reshape(bsz, c, q, G, N)
    acum = jnp.cumsum(dts * a.reshape(G, R), axis=2)
    seg = acum[:, :, :, None] - acum[:, :, None]
    tri = jnp.tril(jnp.ones((q, q), dtype=bool))[:, :, None, None]
    decay = jnp.exp(jnp.where(tri, seg, -jnp.inf))
    cb = jnp.einsum('bctgn,bcsgn->bctsg', cs, bs)
    y_diag = jnp.einsum('bctsg,bctsgr,bcsgr,bcsgrp->bctgrp', cb, decay, dts, xs)
    to_end = jnp.exp(acum[:, :, -1:] - acum)
    chunk_states = jnp.einsum('bcsgn,bcsgr,bcsgrp->bcgrpn', bs, to_end * dts, xs)
    chunk_decay = jnp.exp(acum[:, :, -1])

    def step(h, inp):
        st, dec = inp
        return dec[..., None, None] * h + st, h

    h_final, h_in = lax.scan(step, h0.reshape(bsz, G, R, P, N),
                             (jnp.moveaxis(chunk_states, 1, 0), jnp.moveaxis(chunk_decay, 1, 0)))
    h_in = jnp.moveaxis(h_in, 0, 1)
    y_off = jnp.einsum('bctgn,bctgr,bcgrpn->bctgrp', cs, jnp.exp(acum), h_in)
    y = (y_diag + y_off).reshape(bsz, l, N_SSM_HEADS, P)
    return y, h_final.reshape(bsz, N_SSM_HEADS, P, N)


def mixer(h, k_past, v_past, ssm_h0, conv_prev, p):
    b, l, _ = h.shape
    proj = h @ p['w_in']
    o = 0
    q = proj[..., o:o + D_ATT].reshape(b, l, N_ATT_HEADS, ATT_HEAD_DIM); o += D_ATT
    k = proj[..., o:o + D_ATT].reshape(b, l, N_ATT_HEADS, ATT_HEAD_DIM); o += D_ATT
    v = proj[..., o:o + D_ATT].reshape(b, l, N_ATT_HEADS, ATT_HEAD_DIM); o += D_ATT
    z = proj[..., o:o + D_SSM]; o += D_SSM
    xbc = proj[..., o:o + CONV_DIM]; o += CONV_DIM
    dt_raw = proj[..., o:o + N_SSM_HEADS]

    past = k_past.shape[1]
    k_all = jnp.concatenate([k_past.astype(k.dtype), k], axis=1)
    v_all = jnp.concatenate([v_past.astype(v.dtype), v], axis=1)
    q_pos = past + jnp.arange(l)
    k_pos = jnp.arange(past + l)
    attn = stick_breaking(q, q_pos, k_all, v_all, k_pos).reshape(b, l, D_ATT)

    xbc_c, conv_new = causal_dwconv(xbc, conv_prev, p['ssm_conv_w'], p['ssm_conv_b'])
    xbc_c = jax.nn.silu(xbc_c)
    gn = SSM_GROUPS * SSM_STATE
    xh = xbc_c[..., :D_SSM].reshape(b, l, N_SSM_HEADS, SSM_HEAD_DIM).astype(jnp.float32)
    bm = xbc_c[..., D_SSM:D_SSM + gn].reshape(b, l, SSM_GROUPS, SSM_STATE).astype(jnp.float32)
    cm = xbc_c[..., D_SSM + gn:].reshape(b, l, SSM_GROUPS, SSM_STATE).astype(jnp.float32)
    dt = jax.nn.softplus(dt_raw.astype(jnp.float32) + p['dt_bias'].astype(jnp.float32))
    a = -jnp.exp(p['a_log'].astype(jnp.float32))
    y, h_fin = ssd_scan(xh, dt, a, bm, cm, ssm_h0.astype(jnp.float32))
    y = y + p['d_skip'].astype(jnp.float32)[:, None] * xh
    y = y.reshape(b, l, D_SSM).astype(h.dtype)
    y = rmsnorm(y * jax.nn.silu(z), p['g_ssm_out'])

    out = jnp.concatenate([rmsnorm(attn, p['g_attn_out']), y], axis=-1) @ p['w_out']
    return out, k, v, h_fin.astype(ssm_h0.dtype), conv_new


def conv_ffn(h, conv_prev, p):
    gu = h @ p['w_up']
    gate, up = gu[..., :D_FF], gu[..., D_FF:]
    gate_c, conv_new = causal_dwconv(gate, conv_prev, p['ffn_conv_w'], p['ffn_conv_b'])
    act = jax.nn.gelu(gate_c, approximate=True) * up
    return act @ p['w_down'], conv_new


def layer(x, k_past, v_past, ssm_h0, ssm_conv_prev, ffn_conv_prev, p):
    m, k, v, h_fin, ssm_conv_new = mixer(rmsnorm(x, p['g_mix_pre']), k_past, v_past, ssm_h0, ssm_conv_prev, p)
    x = x + rmsnorm(m, p['g_mix_post'])
    f, ffn_conv_new = conv_ffn(rmsnorm(x, p['g_ffn_pre']), ffn_conv_prev, p)
    x = x + rmsnorm(f, p['g_ffn_post'])
    return x, k, v, h_fin, ssm_conv_new, ffn_conv_new


def setup_inputs(seed: int = 0) -> dict:
    key = jax.random.key(seed)
    ks = jax.random.split(key, 24)
    f32 = jnp.float32

    def nrm(k, shape, scale):
        return jax.random.normal(k, shape, f32) * scale

    def gain(k, shape):
        return 1.0 + 0.05 * jax.random.normal(k, shape, f32)

    dt0 = jnp.exp(jax.random.uniform(ks[12], (DEPTH, N_SSM_HEADS), f32)
                  * (math.log(0.1) - math.log(0.001)) + math.log(0.001))
    dt_bias = dt0 + jnp.log(-jnp.expm1(-dt0))
    a_log = jnp.log(jax.random.uniform(ks[13], (DEPTH, N_SSM_HEADS), f32, minval=1.0, maxval=16.0))
    return {
        'x_prompt': nrm(ks[0], (BATCH, SEQ, D_MODEL), 1.0),
        'x_sample': nrm(ks[1], (DEC_BATCH, DEC_SEQ, D_MODEL), 1.0),
        'cache_k': nrm(ks[2], (DEPTH, DEC_BATCH, PAST_LEN, N_ATT_HEADS, ATT_HEAD_DIM), 1.0),
        'cache_v': nrm(ks[3], (DEPTH, DEC_BATCH, PAST_LEN, N_ATT_HEADS, ATT_HEAD_DIM), 1.0),
        'state_ssm': nrm(ks[4], (DEPTH, DEC_BATCH, N_SSM_HEADS, SSM_HEAD_DIM, SSM_STATE), 0.1),
        'state_ssm_conv': nrm(ks[5], (DEPTH, DEC_BATCH, SSM_CONV - 1, CONV_DIM), 1.0),
        'state_ffn_conv': nrm(ks[6], (DEPTH, DEC_BATCH, FFN_CONV - 1, D_FF), 1.0),
        'g_mix_pre': gain(ks[7], (DEPTH, D_MODEL)),
        'g_mix_post': gain(ks[8], (DEPTH, D_MODEL)),
        'w_in': nrm(ks[9], (DEPTH, D_MODEL, D_IN_PROJ), D_MODEL ** -0.5),
        'ssm_conv_w': nrm(ks[10], (DEPTH, SSM_CONV, CONV_DIM), SSM_CONV ** -0.5),
        'ssm_conv_b': nrm(ks[11], (DEPTH, CONV_DIM), 0.02),
        'dt_bias': dt_bias,
        'a_log': a_log,
        'd_skip': 1.0 + 0.1 * jax.random.normal(ks[14], (DEPTH, N_SSM_HEADS), f32),
        'g_ssm_out': gain(ks[15], (DEPTH, D_SSM)),
        'g_attn_out': gain(ks[16], (DEPTH, D_ATT)),
        'w_out': nrm(ks[17], (DEPTH, D_MIX, D_MODEL), D_MIX ** -0.5),
        'g_ffn_pre': gain(ks[18], (DEPTH, D_MODEL)),
        'g_ffn_post': gain(ks[19], (DEPTH, D_MODEL)),
        'w_up': nrm(ks[20], (DEPTH, D_MODEL, 2 * D_FF), D_MODEL ** -0.5),
        'ffn_conv_w': nrm(ks[21], (DEPTH, FFN_CONV, D_FF), FFN_CONV ** -0.5),
        'ffn_conv_b': nrm(ks[22], (DEPTH, D_FF), 0.02),
        'w_down': nrm(ks[23], (DEPTH, D_FF, D_MODEL), D_FF ** -0.5),
    }


def reference(x_prompt, x_sample, cache_k, cache_v, state_ssm, state_ssm_conv, state_ffn_conv,
              g_mix_pre, g_mix_post, w_in, ssm_conv_w, ssm_conv_b, dt_bias, a_log, d_skip,
              g_ssm_out, g_attn_out, w_out, g_ffn_pre, g_ffn_post, w_up, ffn_conv_w, ffn_conv_b, w_down):
    bp = x_prompt.shape[0]
    dtp = x_prompt.dtype
    zk = jnp.zeros((bp, 0, N_ATT_HEADS, ATT_HEAD_DIM), dtp)
    zh = jnp.zeros((bp, N_SSM_HEADS, SSM_HEAD_DIM, SSM_STATE), dtp)
    zcs = jnp.zeros((bp, SSM_CONV - 1, CONV_DIM), dtp)
    zcf = jnp.zeros((bp, FFN_CONV - 1, D_FF), dtp)

    y_p, y_s = x_prompt, x_sample
    kp_l, vp_l, hp_l, csp_l, cfp_l = [], [], [], [], []
    ks_l, vs_l, hs_l, css_l, cfs_l = [], [], [], [], []
    for i in range(DEPTH):
        p = {
            'g_mix_pre': g_mix_pre[i], 'g_mix_post': g_mix_post[i], 'w_in': w_in[i],
            'ssm_conv_w': ssm_conv_w[i], 'ssm_conv_b': ssm_conv_b[i], 'dt_bias': dt_bias[i],
            'a_log': a_log[i], 'd_skip': d_skip[i], 'g_ssm_out': g_ssm_out[i],
            'g_attn_out': g_attn_out[i], 'w_out': w_out[i], 'g_ffn_pre': g_ffn_pre[i],
            'g_ffn_post': g_ffn_post[i], 'w_up': w_up[i], 'ffn_conv_w': ffn_conv_w[i],
            'ffn_conv_b': ffn_conv_b[i], 'w_down': w_down[i],
        }
        y_p, kp, vp, hp, csp, cfp = layer(y_p, zk, zk, zh, zcs, zcf, p)
        y_s, kn, vn, hn, csn, cfn = layer(y_s, cache_k[i], cache_v[i], state_ssm[i],
                                          state_ssm_conv[i], state_ffn_conv[i], p)
        kp_l.append(kp); vp_l.append(vp); hp_l.append(hp); csp_l.append(csp); cfp_l.append(cfp)
        ks_l.append(kn); vs_l.append(vn); hs_l.append(hn); css_l.append(csn); cfs_l.append(cfn)

    return (y_p, y_s,
            jnp.stack(kp_l), jnp.stack(vp_l), jnp.stack(hp_l), jnp.stack(csp_l), jnp.stack(cfp_l),
            jnp.stack(ks_l), jnp.stack(vs_l), jnp.stack(hs_l), jnp.stack(css_l), jnp.stack(cfs_l))
```

```python
import numpy as np
import ml_dtypes
from contextlib import ExitStack
import concourse.bass as bass
import concourse.mybir as mybir
from concourse.bass_utils import run_bass_kernel_spmd

F32 = mybir.dt.float32
BF16 = mybir.dt.bfloat16
AF = mybir.ActivationFunctionType
ALU = mybir.AluOpType

D = 1024
DATT = 512
NH = 8
HD = 64
DSSM = 512
NST = 64
CONV_DIM = 768
DFF = 2816
NFF = DFF // 128
DIN = 2824
EPS = 1e-6
ENGS = ["pe", "act", "dve", "pool", "sp"]
import os as _os
STOP = int(_os.environ.get("KSTOP", "99"))


_STOPPED = [False]


_GTI = [0]


def chk(k):
    if STOP == k:
        _STOPPED[0] = True
    if STOP == 1000 + k and _GTI[0] == 1:
        _STOPPED[0] = True


class Prog:
    def __init__(self, nc, stack):
        self.nc = nc
        self.stack = stack
        self.ops = {e: [] for e in ENGS}
        self.esem = {e: stack.enter_context(nc.semaphore("s_" + e)) for e in ENGS}
        self.dsem = {}
        self.lastw = {}
        self.readers = {}
        self.known = {e: {} for e in ENGS}
        self.emitted = {e: 0 for e in ENGS}
        self.sigbase = {e: 0 for e in ENGS}
        self.eng_obj = {"pe": nc.tensor, "act": nc.scalar, "dve": nc.vector, "pool": nc.gpsimd, "sp": nc.sync}

    def _need(self, eng, tok, waits):
        if tok is None:
            return
        if tok[0] == "e":
            _, x, idx = tok
            if self.known[eng].get(("e", x), 0) >= idx:
                return
            self.known[eng][("e", x)] = idx
            self.ops[x][idx - 1]["signal"] = True
        else:
            _, key, val = tok
            if self.known[eng].get(("d", key), 0) >= val:
                return
            self.known[eng][("d", key)] = val
        waits.append(tok)

    def op(self, eng, fn, reads=(), writes=(), dma_key=None, pe_acc=False):
        if _STOPPED[0]:
            return None
        waits = []
        for r in reads:
            self._need(eng, self.lastw.get(r), waits)
        for w in writes:
            lw = self.lastw.get(w)
            if not (pe_acc and lw is not None and lw[0] == "e" and lw[1] == "pe" and eng == "pe"):
                self._need(eng, lw, waits)
            for t in self.readers.get(w, ()):
                self._need(eng, t, waits)
        idx = len(self.ops[eng]) + 1
        rec = {"fn": fn, "waits": waits, "signal": False, "dma": None}
        if dma_key is not None:
            if dma_key not in self.dsem:
                self.dsem[dma_key] = [self.stack.enter_context(self.nc.semaphore("d%d" % len(self.dsem))), 0]
            self.dsem[dma_key][1] += 16
            rec["dma"] = dma_key
            tok = ("d", dma_key, self.dsem[dma_key][1])
        else:
            tok = ("e", eng, idx)
        self.ops[eng].append(rec)
        for w in writes:
            self.lastw[w] = tok
            self.readers[w] = []
        for r in reads:
            if r not in writes:
                self.readers.setdefault(r, []).append(tok)
        return tok

    def barrier(self, final=False):
        lasts = {}
        for e in ENGS:
            k = len(self.ops[e])
            while k > 0 and (self.ops[e][k - 1]["fn"] is None or self.ops[e][k - 1]["dma"] is not None):
                k -= 1
            lasts[e] = k
        dvals = {k: v[1] for k, v in self.dsem.items()}
        for e in (["sp"] if final else ENGS):
            waits = []
            for x in ENGS:
                if x != e and lasts[x] > 0:
                    self._need(e, ("e", x, lasts[x]), waits)
            for k, v in dvals.items():
                if v > 0:
                    self._need(e, ("d", k, v), waits)
            self.ops[e].append({"fn": None, "waits": waits, "signal": False, "dma": None})
        if not final:
            self.lastw.clear()
            self.readers.clear()

    def emit(self):
        pref = {}
        for e in ENGS:
            c = 0
            arr = []
            for o in self.ops[e]:
                if o["signal"] and o["dma"] is None and o["fn"] is not None:
                    c += 1
                arr.append(c)
            pref[e] = arr
        start = dict(self.emitted)

        def run(e, engine):
            ops = self.ops[e]
            for i in range(start[e], len(ops)):
                o = ops[i]
                for t in o["waits"]:
                    if t[0] == "e":
                        engine.wait_ge(self.esem[t[1]], pref[t[1]][t[2] - 1])
                    else:
                        engine.wait_ge(self.dsem[t[1]][0], t[2])
                if o["fn"] is None:
                    continue
                ins = o["fn"](engine)
                if o["dma"] is not None:
                    ins.then_inc(self.dsem[o["dma"]][0], 16)
                elif o["signal"]:
                    ins.then_inc(self.esem[e], 1)

        with self.nc.Block() as block:
            @block.tensor
            def _(t):
                run("pe", t)

            @block.scalar
            def _(a):
                run("act", a)

            @block.vector
            def _(v):
                run("dve", v)

            @block.gpsimd
            def _(g):
                run("pool", g)

            @block.sync
            def _(s):
                run("sp", s)
        for e in ENGS:
            self.emitted[e] = len(self.ops[e])


def build(NP=2048, PAST=2048, NS=32, NSEQ=2):
    nc = bass.Bass("TRN2", target_bir_lowering=False)
    KMAX = max(NP, PAST + NS)
    NKT = (KMAX + 127) // 128
    st = ExitStack()
    with st:
        def din(name, shape, dt=F32):
            return nc.dram_tensor(name, list(shape), dt, kind="ExternalInput").ap()

        def dout(name, shape):
            return nc.dram_tensor(name, list(shape), F32, kind="ExternalOutput").ap()

        xp = din("xp", [NP, D]); xs = din("xs", [NSEQ * NS, D])
        ck = din("ck", [NSEQ, PAST, DATT]); cv = din("cv", [NSEQ, PAST, DATT])
        sst = din("sst", [NSEQ, NST, DSSM]); scv = din("scv", [NSEQ, 128, 8, 3]); sfc = din("sfc", [NSEQ, 128, NFF, 2])
        w_in = din("w_in", [D, DIN]); w_out = din("w_out", [D, D]); w_up = din("w_up", [D, 2 * DFF]); w_dn = din("w_dn", [DFF, D])
        gvec = din("gvec", [1, 5120]); hvec = din("hvec", [1, 24])
        wcs = din("wcs", [128, 8, 5]); wcf = din("wcf", [128, NFF, 4]); cst = din("cst", [128, 5 * 128])
        yp = dout("yp", [NP, D]); ys = dout("ys", [NSEQ * NS, D])
        kTp = dout("kTp", [NH, HD, NP]); vp = dout("vp", [NP, DATT])
        kTs = dout("kTs", [NSEQ, NH, HD, NS]); vs = dout("vs", [NSEQ * NS, DATT])
        ssp = dout("ssp", [NST, DSSM]); sss = dout("sss", [NSEQ, NST, DSSM])
        scp = dout("scp", [128, 8, 3]); scs = dout("scs", [NSEQ, 128, 8, 3])
        fcp = dout("fcp", [128, NFF, 2]); fcs = dout("fcs", [NSEQ, 128, NFF, 2])

        pr = Prog(nc, st)

        def sb(name, shape, dt=F32):
            return st.enter_context(nc.sbuf_tensor(name, list(shape), dt))

        def ps(name, shape, dt=F32):
            return st.enter_context(nc.psum_tensor(name, list(shape), dt))

        def mm(out, lhsT, rhs, reads, writes, start=True, stop=True, acc=False, **kw):
            pr.op("pe", lambda e: e.matmul(out, lhsT=lhsT, rhs=rhs, start=start, stop=stop, **kw), reads, writes, pe_acc=acc)

        def tr(out, in_, ident, reads, writes, acc=False):
            pr.op("pe", lambda e: e.transpose(out, in_, ident), reads, writes, pe_acc=acc)

        def act(out, in_, func, reads, writes, **kw):
            pr.op("act", lambda e: e.activation(out=out, in_=in_, func=func, **kw), reads, writes)

        def tt(eng, out, in0, in1, op, reads, writes):
            pr.op(eng, lambda e: e.tensor_tensor(out=out, in0=in0, in1=in1, op=op), reads, writes)

        def ts(eng, out, in0, s1, s2, op0, op1, reads, writes):
            if s2 is None:
                pr.op(eng, lambda e: e.tensor_scalar(out=out, in0=in0, scalar1=s1, scalar2=None, op0=op0), reads, writes)
            else:
                pr.op(eng, lambda e: e.tensor_scalar(out=out, in0=in0, scalar1=s1, scalar2=s2, op0=op0, op1=op1), reads, writes)

        def stt(out, in0, scalar, in1, op0, op1, reads, writes):
            pr.op("dve", lambda e: e.scalar_tensor_tensor(out=out, in0=in0, scalar=scalar, in1=in1, op0=op0, op1=op1), reads, writes)

        def cp(eng, out, in_, reads, writes):
            if eng == "act":
                pr.op("act", lambda e: e.copy(out=out, in_=in_), reads, writes)
            else:
                pr.op(eng, lambda e: e.tensor_copy(out=out, in_=in_), reads, writes)

        def mset(eng, ap, val, writes):
            pr.op(eng, lambda e: e.memset(ap, val), (), writes)

        def dma(q, out, in_, reads, writes, key, **kw):
            pr.op(q, lambda e: e.dma_start(out=out, in_=in_, **kw), reads, writes, dma_key=key)

        cst_f = sb("cst_f", [128, 5 * 128])
        ident_f = cst_f[:, 0:128]; triincl = cst_f[:, 128:256]; ones_f = cst_f[:, 256:384]; negmask = cst_f[:, 384:512]
        cst_b = sb("cst_b", [128, 4 * 128], BF16)
        ident_b = cst_b[:, 0:128]; negtri = cst_b[:, 128:256]; negones = cst_b[:, 256:384]; mask01 = cst_b[:, 384:512]
        cstb_d = din("cstb", [128, 4 * 128])
        hv = sb("hv", [128, 24])
        a_bc = sb("a_bc", [128, 8]); d_bc = sb("d_bc", [128, 512])
        wc_s = sb("wc_s", [128, 8, 5]); wc_f = sb("wc_f", [128, NFF, 4])

        dma("sp", cst_f[:], cst[:, :], (), ["cst_f"], "cst_f")
        dma("pool", cst_b[:], cstb_d[:, :], (), ["cst_b"], "cst_b")
        dma("sp", hv[:].unsqueeze(1), hvec.partition_broadcast(128), (), ["hv"], "hv")
        dma("sp", wc_s[:], wcs[:, :, :], (), ["wc_s"], "wc_s")
        dma("sp", wc_f[:], wcf[:, :, :], (), ["wc_f"], "wc_f")
        act(a_bc[:], hv[:, 8:16], AF.Exp, ["hv"], ["a_bc"])
        ts("dve", a_bc[:], a_bc[:], -1.0, None, ALU.mult, None, ["a_bc"], ["a_bc"])
        cp("dve", d_bc[:].rearrange("p (h d) -> p h d", h=8), hv[:, 16:24].unsqueeze(2).to_broadcast([128, 8, 64]), ["hv"], ["d_bc"])

        psA = ps("psA", [128, 512]); psB = ps("psB", [128, 512])
        psS = [ps("psS0", [128, 512]), ps("psS1", [128, 512])]
        psL = [ps("psL0", [128, 512]), ps("psL1", [128, 512])]
        psO = ps("psO", [128, 512])
        psT = ps("psT", [128, 1024], BF16)

        seqs = [dict(n=NP, past=0, x=xp, y=yp, row0=0, kT=kTp, v=vp, ss=ssp[:, :], sc=scp[:, :, :], fc=fcp[:, :, :], idx=None)]
        for b in range(NSEQ):
            seqs.append(dict(n=NS, past=PAST, x=xs, y=ys, row0=b * NS, kT=kTs[b], v=vs, ss=sss[b], sc=scs[b], fc=fcs[b], idx=b))

        p1 = ExitStack()
        with p1:
            def sb1(name, shape, dt=F32):
                return p1.enter_context(nc.sbuf_tensor(name, list(shape), dt))

            gbc = sb1("gbc", [128, 3072])
            g_pre = gbc[:, 0:1024]; g_post = gbc[:, 1024:2048]; g_ssm = gbc[:, 2048:2560]; g_attn = gbc[:, 2560:3072]
            dma("sp", gbc[:].unsqueeze(1), gvec[:, 0:3072].partition_broadcast(128), (), ["gbc"], "gbc")
            win = sb1("win", [128, 8, DIN], BF16)
            wo = sb1("wo", [128, 8, D], BF16)
            for bi_, (ca_, cb_) in enumerate([(0, 1024), (1024, 2048), (2048, DIN)]):
                dma("pool", win[:, :, ca_:cb_], w_in[:, ca_:cb_].rearrange("(c p) n -> p c n", p=128), (), [("win", bi_)], "win%d" % bi_, max_dma_last_dim=4096)
            for c in range(8):
                dma("pool", wo[:, c, :], w_out[c * 128:(c + 1) * 128, :], (), ["wo"], "wo", max_dma_last_dim=4096)

            KT = sb1("KT", [64, 8, NKT * 128], BF16)
            V = sb1("V", [128, NKT, 512], BF16)
            xblk = [sb1("xblk0", [128, D]), sb1("xblk1", [128, D])]
            xn = sb1("xn", [128, D], BF16)
            xnT = sb1("xnT", [128, 8, 128], BF16)
            qT = sb1("qT", [64, 8, 128], BF16)
            kst = sb1("kst", [64, 8, 128])
            vst = sb1("vst", [128, 512])
            xbcT = sb1("xbcT", [128, 8, 131])
            xcT = sb1("xcT", [128, 8, 128])
            BCTb = sb1("BCTb", [64, 4, 128], BF16)
            col = sb1("col", [128, 12])
            dtt = sb1("dtt", [128, 8]); dA = sb1("dA", [128, 8]); cum = sb1("cum", [128, 8]); te = sb1("te", [128, 8])
            e_t = [sb1("e0", [128, 512]), sb1("e1", [128, 512])]
            sp_t = [sb1("sp0", [128, 512]), sb1("sp1", [128, 512])]
            spb_t = [sb1("spb0", [128, 512], BF16), sb1("spb1", [128, 512], BF16)]
            w_t = [sb1("w0", [128, 512], BF16), sb1("w1", [128, 512], BF16)]
            Racc = [sb1("Racc0", [128, 512], BF16), sb1("Racc1", [128, 512], BF16)]
            zb = sb1("zb", [128, 512], BF16)
            attn_n = sb1("attn_n", [128, 512], BF16)
            attn_nT = sb1("attn_nT", [128, 4, 128], BF16)
            segb = sb1("segb", [128, 8, 128])
            MT = sb1("MT", [128, 8, 128], BF16)
            E_bc = sb1("E_bc", [64, 8, 128])
            CsT = sb1("CsT", [64, 8, 128], BF16)
            x_tok = sb1("x_tok", [128, 512]); xdt = sb1("xdt", [128, 512], BF16); xw = sb1("xw", [128, 512], BF16)
            B_tok = sb1("B_tok", [128, 128], BF16)
            hT = sb1("hT", [64, 8, 64]); hTb = sb1("hTb", [64, 8, 64], BF16); htmp = sb1("htmp", [64, 8, 64])
            sz = sb1("sz", [128, 512]); yf = sb1("yf", [128, 512]); yn = sb1("yn", [128, 512], BF16)
            ynT = sb1("ynT", [128, 4, 128], BF16)
            tmpo = [sb1("tmpo0", [128, D])] * 2
            kcs = vst; kcb = xdt

            if _os.environ.get("KDEBUG"):
                print("phase1 sbuf remaining", nc.sbuf_bytes_remaining)
            mset("dve", xbcT[:], 0.0, ["xbcT"])
            mset("dve", xcT[:], 0.0, [("xcT", c) for c in range(8)])
            mset("pool", KT[:], 0.0, [("KT", i) for i in range(NKT)])
            mset("pool", zb[:], 0.0, ["zb"])

            gti = 0
            chk(0)
            all_tiles = []
            for S_ in seqs:
                P_ = min(128, S_["n"])
                for ti_ in range(S_["n"] // P_):
                    all_tiles.append((S_, ti_, P_))

            def emit_xload(g):
                if g >= len(all_tiles):
                    return
                S_, ti_, P_ = all_tiles[g]
                r0_ = S_["row0"] + ti_ * P_
                dma("sp", xblk[g % 2][0:P_, :], S_["x"][r0_:r0_ + P_, :], (), ["xblk%d" % (g % 2)], "xblk%d" % (g % 2))
            for sq_i, S in enumerate(seqs):
                n = S["n"]; past = S["past"]
                P = min(128, n); nt = n // P
                if S["idx"] is None:
                    mset("dve", hT[:], 0.0, ["hT"])
                    mset("pool", hTb[:], 0.0, ["hTb"])
                    mset("dve", xbcT[:, :, 0:3], 0.0, ["xbcT"])
                else:
                    b = S["idx"]
                    dma("sp", hT[:].rearrange("n h p -> n (h p)"), sst[b], (), ["hT"], "hT")
                    cp("pool", hTb[:], hT[:], ["hT"], ["hTb"])
                    dma("sp", xbcT[:, :, 0:3], scv[b], (), ["xbcT"], "xbcT")
                    for kt in range(past // 128):
                        kb_, kn_ = [(vst, "vst"), (x_tok, "x_tok")][kt % 2]
                        vb_, vn_ = [(yf, "yf"), (sz, "sz")][kt % 2]
                        dma("sp", kb_[:], ck[b, kt * 128:(kt + 1) * 128, :], (), [kn_], kn_)
                        dma("sp", vb_[:], cv[b, kt * 128:(kt + 1) * 128, :], (), [vn_], vn_)
                        cp("dve", kcb[:], kb_[:], [kn_], ["xdt"])
                        for h in range(8):
                            tr(psT[0:64, h * 128:(h + 1) * 128], kcb[:, h * 64:(h + 1) * 64], ident_b, ["xdt", "cst_b"], ["psT"], acc=(h > 0))
                        cp("act", KT[:, :, kt * 128:(kt + 1) * 128], psT[0:64, :].rearrange("p (h t) -> p h t", h=8), ["psT"], [("KT", kt)])
                        cp("dve", V[:, kt, :], vb_[:], [vn_], [("V", kt)])

                for ti in range(nt):
                    t0 = ti * P
                    kp0 = past + t0
                    _GTI[0] = gti
                    ktile = kp0 // 128
                    krow = kp0 % 128
                    assert krow == 0
                    xb = xblk[gti % 2]; xbn = "xblk%d" % (gti % 2)
                    r0 = S["row0"] + t0
                    if gti == 0:
                        emit_xload(0)
                    emit_xload(gti + 1)
                    chk(1)
                    act(xn[0:P, :], xb[0:P, :], AF.Square, [xbn], ["xn", "col"], accum_out=col[0:P, 0:1])
                    act(col[0:P, 1:2], col[0:P, 0:1], AF.Ln, ["col"], ["col"], scale=1.0 / D, bias=EPS)
                    act(col[0:P, 2:3], col[0:P, 1:2], AF.Exp, ["col"], ["col"], scale=-0.5)
                    stt(xn[0:P, :], xb[0:P, :], col[0:P, 2:3], g_pre[0:P, :], ALU.mult, ALU.mult, [xbn, "col", "gbc"], ["xn"])
                    chk(2)
                    for dc in range(8):
                        tr(psT[:, dc * 128:dc * 128 + P], xn[0:P, dc * 128:(dc + 1) * 128], ident_b[0:P, 0:P], ["xn", "cst_b"], ["psT"], acc=(dc > 0))
                    cp("act", xnT[:, :, 0:P], psT[:, :].rearrange("p (c t) -> p c t", c=8)[:, :, 0:P], ["psT"], ["xnT"])
                    chk(3)
                    chk(34)
                    for c in range(4):
                        c0 = 2048 + c * 128
                        for dc in range(8):
                            mm(psA[:, c * 128:c * 128 + P], win[:, dc, c0:c0 + 128], xnT[:, dc, 0:P], [("win", 2), "xnT"], ["psA"],
                               start=(dc == 0), stop=(dc == 7), acc=(dc > 0 or c > 0))
                    cp("act", xbcT[:, 0:4, 3:3 + P], psA[:, :].rearrange("p (c t) -> p c t", c=4)[:, :, 0:P], ["psA"], ["xbcT"])
                    for c in range(4):
                        c0 = 2560 + c * 64
                        for dc in range(8):
                            mm(psB[0:64, c * 128:c * 128 + P], win[:, dc, c0:c0 + 64], xnT[:, dc, 0:P], [("win", 2), "xnT"], ["psB"],
                               start=(dc == 0), stop=(dc == 7), acc=(dc > 0 or c > 0))
                    cp("act", xbcT[0:64, 4:8, 3:3 + P], psB[0:64, :].rearrange("p (c t) -> p c t", c=4)[:, :, 0:P], ["psB"], ["xbcT"])
                    chk(35)
                    for dc in range(8):
                        mm(psL[0][0:P, 0:8], xnT[:, dc, 0:P], win[:, dc, 2816:2824], [("win", 2), "xnT"], ["psL0"], start=(dc == 0), stop=(dc == 7), acc=(dc > 0))
                    tt("dve", dtt[0:P, :], psL[0][0:P, 0:8], hv[0:P, 0:8], ALU.add, ["psL0", "hv"], ["dtt"])
                    act(dtt[0:P, :], dtt[0:P, :], AF.Exp, ["dtt"], ["dtt"])
                    act(dtt[0:P, :], dtt[0:P, :], AF.Ln, ["dtt"], ["dtt"], bias=1.0)
                    tt("dve", dA[0:P, :], dtt[0:P, :], a_bc[0:P, :], ALU.mult, ["dtt", "a_bc"], ["dA"])

                    chk(4)
                    for c in range(8):
                        M = 128 if c < 4 else 64
                        ts("dve", xcT[0:M, c, 0:P], xbcT[0:M, c, 3:3 + P], wc_s[0:M, c, 3:4], wc_s[0:M, c, 4:5], ALU.mult, ALU.add,
                           ["xbcT", "wc_s"], [("xcT", c)])
                    for i in range(3):
                        for c in range(8):
                            M = 128 if c < 4 else 64
                            stt(xcT[0:M, c, 0:P], xbcT[0:M, c, i:i + P], wc_s[0:M, c, i:i + 1], xcT[0:M, c, 0:P], ALU.mult, ALU.add,
                                ["xbcT", "wc_s", ("xcT", c)], [("xcT", c)])
                    for grp in range(4):
                        bank = psA if grp % 2 == 0 else psB
                        bname = "psA" if grp % 2 == 0 else "psB"
                        for j in range(4):
                            c0 = (grp * 4 + j) * 64
                            for dc in range(8):
                                mm(bank[0:64, j * 128:j * 128 + P], win[:, dc, c0:c0 + 64], xnT[:, dc, 0:P], [("win", 0), "xnT"], [bname],
                                   start=(dc == 0), stop=(dc == 7), acc=(dc > 0 or j > 0))
                        src = bank[0:64, :].rearrange("p (j t) -> p j t", j=4)[:, :, 0:P]
                        if grp < 2:
                            act(qT[:, grp * 4:(grp + 1) * 4, 0:P], src, AF.Copy, [bname], ["qT"], scale=0.125)
                        else:
                            g2 = grp - 2
                            cp("act", KT[:, g2 * 4:(g2 + 1) * 4, kp0:kp0 + P], src, [bname], [("KT", ktile)])
                            cp("act", kst[:, g2 * 4:(g2 + 1) * 4, 0:P], src, [bname], ["kst"])
                    chk(31)
                    dma("sp", S["kT"][:, :, t0:t0 + P].rearrange("h d t -> d h t"), kst[:, :, 0:P], ["kst"], [], "kst_o")
                    chk(32)
                    for dc in range(8):
                        mm(psA[0:P, :], xnT[:, dc, 0:P], win[:, dc, 1024:1536], [("win", 1), "xnT"], ["psA"], start=(dc == 0), stop=(dc == 7), acc=(dc > 0))
                    chk(321)
                    cp("act", V[0:P, ktile, :], psA[0:P, :], ["psA"], [("V", ktile)])
                    chk(322)
                    cp("act", vst[0:P, :], psA[0:P, :], ["psA"], ["vst"])
                    chk(323)
                    dma("sp", S["v"][r0:r0 + P, :], vst[0:P, :], ["vst"], [], "vst_o")
                    chk(33)
                    for dc in range(8):
                        mm(psB[0:P, :], xnT[:, dc, 0:P], win[:, dc, 1536:2048], [("win", 1), "xnT"], ["psB"], start=(dc == 0), stop=(dc == 7), acc=(dc > 0))
                    act(sz[0:P, :], psB[0:P, :], AF.Silu, ["psB"], ["sz"])
                    xc_all = [("xcT", c) for c in range(8)]
                    act(xcT[:, :, 0:P], xcT[:, :, 0:P], AF.Silu, xc_all, xc_all)
                    cp("dve", xbcT[:, :, 0:3], xbcT[:, :, P:P + 3], ["xbcT"], ["xbcT"])
                    cp("dve", BCTb[:, :, 0:P], xcT[0:64, 4:8, 0:P], xc_all, ["BCTb"])

                    chk(5)
                    ktl = []
                    for kt in range(ktile + 1):
                        ktl.append((kt, P if kt == ktile else 128, kt == ktile))
                    ktl = ktl[::-1]
                    HPU = 8 if P <= 64 else 4
                    NG = 8 // HPU
                    W = HPU * P
                    units = []
                    for li, (kt, nk, diag) in enumerate(ktl):
                        for grp in range(NG):
                            units.append((li, kt, nk, diag, grp, len(units) % 2))
                    for grp in range(NG):
                        mset("pool", Racc[grp][:, 0:W], 0.0, ["Racc%d" % grp])
                    mm(psO[0:P, :], zb[:, 0:P], zb[:, :], ["zb"], ["psO"], start=True, stop=False)

                    nkt_ = len(ktl)

                    def tbuf(par):
                        return segb[:, 4 * par:4 * par + 4, :].rearrange("k r t -> k (r t)")

                    def names(grp, par):
                        return ("psS%d" % par, "psL%d" % par, "e%d" % par, "sp%d" % par, "spb%d" % par, "w%d" % par, "Racc%d" % grp)

                    def st_S(li, kt, nk, diag, grp, par):
                        Sn = names(grp, par)[0]
                        for j in range(HPU):
                            h = HPU * grp + j
                            mm(psS[par][0:nk, j * P:(j + 1) * P], KT[:, h, kt * 128:kt * 128 + nk], qT[:, h, 0:P], [("KT", kt), "qT"], [Sn], acc=(j > 0))

                    def st_E(li, kt, nk, diag, grp, par):
                        Sn, Ln_, en, spn, spbn, wn, rn = names(grp, par)
                        act(e_t[par][0:nk, 0:W], psS[par][0:nk, 0:W], AF.Exp, [Sn], [en])
                        act(sp_t[par][0:nk, 0:W], e_t[par][0:nk, 0:W], AF.Ln, [en], [spn], bias=1.0)

                    def st_C(li, kt, nk, diag, grp, par):
                        Sn, Ln_, en, spn, spbn, wn, rn = names(grp, par)
                        sp_ = sp_t[par][0:nk, 0:W]; spb_ = spb_t[par][0:nk, 0:W]
                        if diag:
                            tt("dve", spb_.rearrange("k (j q) -> k j q", j=HPU), sp_.rearrange("k (j q) -> k j q", j=HPU),
                               mask01[0:nk, 0:P].unsqueeze(1).to_broadcast([nk, HPU, P]), ALU.mult, [spn, "cst_b"], [spbn])
                        else:
                            cp("dve", spb_, sp_, [spn], [spbn])

                    def st_L(li, kt, nk, diag, grp, par):
                        Sn, Ln_, en, spn, spbn, wn, rn = names(grp, par)
                        first = li == 0
                        Lb = psL[par][0:nk, 0:W]; spb_ = spb_t[par][0:nk, 0:W]
                        mm(Lb, negtri[0:nk, 0:nk], spb_, [spbn, "cst_b"], [Ln_], start=True, stop=False)
                        if not first:
                            mm(Lb, negones[:, 0:nk], Racc[grp][:, 0:W], [rn, "cst_b"], [Ln_], start=False, stop=False, acc=True)
                        for j in range(HPU):
                            h = HPU * grp + j
                            mm(psL[par][0:nk, j * P:(j + 1) * P], KT[:, h, kt * 128:kt * 128 + nk], qT[:, h, 0:P], [("KT", kt), "qT"], [Ln_],
                               start=False, stop=(j == HPU - 1), acc=True)

                    def st_RT(li, kt, nk, diag, grp, par):
                        Sn, Ln_, en, spn, spbn, wn, rn = names(grp, par)
                        last = li == nkt_ - 1
                        spb_ = spb_t[par][0:nk, 0:W]
                        if not last:
                            tt("dve", Racc[grp][0:nk, 0:W], Racc[grp][0:nk, 0:W], spb_, ALU.add, [rn, spbn], [rn])
                        tt("dve", tbuf(par)[0:nk, 0:W], psL[par][0:nk, 0:W], sp_t[par][0:nk, 0:W], ALU.subtract, [Ln_, spn], [("segb", par)])

                    def st_W(li, kt, nk, diag, grp, par):
                        Sn, Ln_, en, spn, spbn, wn, rn = names(grp, par)
                        w_ = w_t[par][0:nk, 0:W]
                        act(w_, tbuf(par)[0:nk, 0:W], AF.Exp, [("segb", par)], [wn])
                        if diag:
                            tt("pool", w_.rearrange("k (j q) -> k j q", j=HPU), w_.rearrange("k (j q) -> k j q", j=HPU),
                               mask01[0:nk, 0:P].unsqueeze(1).to_broadcast([nk, HPU, P]), ALU.mult, [wn, "cst_b"], [wn])

                    def st_PV(li, kt, nk, diag, grp, par):
                        wn = names(grp, par)[5]
                        for j in range(HPU):
                            h = HPU * grp + j
                            mm(psO[0:P, h * 64:(h + 1) * 64], w_t[par][0:nk, j * P:(j + 1) * P], V[0:nk, kt, h * 64:(h + 1) * 64],
                               [wn, ("V", kt)], ["psO"], start=False, stop=False, acc=True, skip_group_check=True)

                    U_ = len(units)

                    def run_st(fn, idx):
                        if 0 <= idx < U_:
                            fn(*units[idx])

                    for it in range(-2, U_ + 1):
                        run_st(st_S, it + 2)
                        run_st(st_L, it)
                        run_st(st_PV, it - 1)
                        run_st(st_RT, it)
                        run_st(st_E, it + 2)
                        run_st(st_C, it + 2)
                        run_st(st_W, it)
                    mm(psO[0:P, :], zb[:, 0:P], zb[:, :], ["zb"], ["psO"], start=False, stop=True, acc=True)
                    act(xn[0:P, 0:512], psO[0:P, :], AF.Square, ["psO"], ["xn", "col"], accum_out=col[0:P, 8:9])
                    act(col[0:P, 9:10], col[0:P, 8:9], AF.Ln, ["col"], ["col"], scale=1.0 / DATT, bias=EPS)
                    act(col[0:P, 10:11], col[0:P, 9:10], AF.Exp, ["col"], ["col"], scale=-0.5)
                    stt(attn_n[0:P, :], psO[0:P, :], col[0:P, 10:11], g_attn[0:P, :], ALU.mult, ALU.mult, ["psO", "col", "gbc"], ["attn_n"])
                    for c in range(4):
                        tr(psT[:, c * 128:c * 128 + P], attn_n[0:P, c * 128:(c + 1) * 128], ident_b[0:P, 0:P], ["attn_n", "cst_b"], ["psT"], acc=(c > 0))
                    cp("act", attn_nT[:, :, 0:P], psT[:, 0:512].rearrange("p (c t) -> p c t", c=4)[:, :, 0:P], ["psT"], ["attn_nT"])

                    chk(6)
                    mm(psL[1][0:P, 0:8], triincl[0:P, 0:P], dA[0:P, :], ["dA", "cst_f"], ["psL1"])
                    cp("act", cum[0:P, :], psL[1][0:P, 0:8], ["psL1"], ["cum"])
                    tt("dve", segb[0:P, :, 0:P], dA[0:P, :].unsqueeze(2).to_broadcast([P, 8, P]),
                       triincl[0:P, 0:P].unsqueeze(1).to_broadcast([P, 8, P]), ALU.mult, ["dA", "cst_f"], [("segb", 0), ("segb", 1)])
                    for hf in range(2):
                        mm(psS[hf][:, 0:4 * P], ones_f[0:P, :], segb[0:P, 4 * hf:4 * hf + 4, 0:P], [("segb", 0), ("segb", 1), "cst_f"], ["psS%d" % hf])
                    for c in range(4):
                        tr(psA[0:P, c * 128:(c + 1) * 128], xcT[:, c, 0:P], ident_f, [("xcT", c), "cst_f"], ["psA"], acc=(c > 0))
                    for j in range(2):
                        tr(psB[0:P, j * 64:(j + 1) * 64], xcT[0:64, 4 + j, 0:P], ident_f[0:64, 0:64], [("xcT", 4 + j), "cst_f"], ["psB"], acc=(j > 0))
                    cp("act", x_tok[0:P, :], psA[0:P, :], ["psA"], ["x_tok"])
                    cp("act", B_tok[0:P, :], psB[0:P, 0:128], ["psB"], ["B_tok"])
                    tt("dve", xdt[0:P, :].rearrange("p (h d) -> p h d", h=8), x_tok[0:P, :].rearrange("p (h d) -> p h d", h=8),
                       dtt[0:P, :].unsqueeze(2).to_broadcast([P, 8, 64]), ALU.mult, ["x_tok", "dtt"], ["xdt"])
                    tt("pool", x_tok[0:P, :], x_tok[0:P, :], d_bc[0:P, :], ALU.mult, ["x_tok", "d_bc"], ["x_tok"])
                    for g in range(2):
                        mm(psB[0:P, 256 + g * 128:256 + g * 128 + P], BCTb[:, g, 0:P], BCTb[:, 2 + g, 0:P], ["BCTb"], ["psB"], acc=True)
                    for hf in range(2):
                        pn = "psS%d" % hf
                        ab = psS[hf][:, 0:4 * P].rearrange("p (r t) -> p r t", r=4)
                        hs = slice(4 * hf, 4 * hf + 4)
                        tt("dve", te[0:P, hs], ab[0:P, :, P - 1], cum[0:P, hs], ALU.subtract, [pn, "cum"], [("te", hf)])
                        act(E_bc[:, hs, 0:P], ab[0:64, :, :], AF.Exp, [pn], [("E_bc", hf)])
                        for r_ in range(4):
                            stt(segb[0:P, 4 * hf + r_, 0:P], ab[0:P, r_, :], cum[0:P, 4 * hf + r_:4 * hf + r_ + 1], negmask[0:P, 0:P],
                                ALU.subtract, ALU.add, [pn, "cum", "cst_f"], [("segb", hf)])
                        act(segb[0:P, hs, 0:P], segb[0:P, hs, 0:P], AF.Exp, [("segb", hf)], [("segb", hf)])
                        tt("dve", MT[0:P, hs, 0:P], segb[0:P, hs, 0:P],
                           psB[0:P, 256 + hf * 128:256 + hf * 128 + P].unsqueeze(1).to_broadcast([P, 4, P]), ALU.mult,
                           [("segb", hf), "psB"], [("MT", hf)])
                        tt("dve", CsT[:, hs, 0:P], E_bc[:, hs, 0:P], BCTb[:, 2 + hf, 0:P].unsqueeze(1).to_broadcast([64, 4, P]), ALU.mult,
                           [("E_bc", hf), "BCTb"], [("CsT", hf)])
                    act(te[0:P, :], te[0:P, :], AF.Exp, [("te", 0), ("te", 1)], [("te", 0), ("te", 1)])
                    tt("dve", xw[0:P, :].rearrange("p (h d) -> p h d", h=8), xdt[0:P, :].rearrange("p (h d) -> p h d", h=8),
                       te[0:P, :].unsqueeze(2).to_broadcast([P, 8, 64]), ALU.mult, ["xdt", ("te", 0), ("te", 1)], ["xw"])
                    for h in range(8):
                        hf = h // 4
                        mm(psL[0][0:P, h * 64:(h + 1) * 64], MT[0:P, h, 0:P], xdt[0:P, h * 64:(h + 1) * 64], [("MT", hf), "xdt"], ["psL0"],
                           start=True, stop=False, acc=(h > 0))
                        mm(psL[0][0:P, h * 64:(h + 1) * 64], CsT[:, h, 0:P], hTb[:, h, :], [("CsT", hf), "hTb"], ["psL0"],
                           start=False, stop=True, acc=True)
                    for g in range(2):
                        mm(psL[1][0:64, g * 256:(g + 1) * 256], B_tok[0:P, g * 64:(g + 1) * 64], xw[0:P, g * 256:(g + 1) * 256], ["B_tok", "xw"], ["psL1"],
                           acc=(g > 0))
                    tt("dve", htmp[:], hT[:], E_bc[:, :, P - 1].unsqueeze(2).to_broadcast([64, 8, 64]), ALU.mult,
                       ["hT", ("E_bc", 0), ("E_bc", 1)], ["htmp"])
                    tt("dve", hT[:].rearrange("n h p -> n (h p)"), htmp[:].rearrange("n h p -> n (h p)"), psL[1][0:64, :], ALU.add, ["htmp", "psL1"], ["hT"])
                    cp("pool", hTb[:], hT[:], ["hT"], ["hTb"])
                    tt("dve", yf[0:P, :], psL[0][0:P, :], x_tok[0:P, :], ALU.add, ["psL0", "x_tok"], ["yf"])
                    tt("dve", yf[0:P, :], yf[0:P, :], sz[0:P, :], ALU.mult, ["yf", "sz"], ["yf"])
                    act(xn[0:P, 0:512], yf[0:P, :], AF.Square, ["yf"], ["xn", "col"], accum_out=col[0:P, 3:4])
                    act(col[0:P, 4:5], col[0:P, 3:4], AF.Ln, ["col"], ["col"], scale=1.0 / DSSM, bias=EPS)
                    act(col[0:P, 5:6], col[0:P, 4:5], AF.Exp, ["col"], ["col"], scale=-0.5)
                    stt(yn[0:P, :], yf[0:P, :], col[0:P, 5:6], g_ssm[0:P, :], ALU.mult, ALU.mult, ["yf", "col", "gbc"], ["yn"])
                    for c in range(4):
                        tr(psT[:, c * 128:c * 128 + P], yn[0:P, c * 128:(c + 1) * 128], ident_b[0:P, 0:P], ["yn", "cst_b"], ["psT"], acc=(c > 0))
                    cp("act", ynT[:, :, 0:P], psT[:, 0:512].rearrange("p (c t) -> p c t", c=4)[:, :, 0:P], ["psT"], ["ynT"])

                    chk(7)
                    to = tmpo[0]; ton = "tmpo0"
                    for hf in range(2):
                        bank = psA if hf == 0 else psB
                        bname = "psA" if hf == 0 else "psB"
                        for c in range(4):
                            mm(bank[0:P, :], attn_nT[:, c, 0:P], wo[:, c, hf * 512:(hf + 1) * 512], ["attn_nT", "wo"], [bname],
                               start=(c == 0), stop=False, acc=(c > 0))
                        for c in range(4):
                            mm(bank[0:P, :], ynT[:, c, 0:P], wo[:, 4 + c, hf * 512:(hf + 1) * 512], ["ynT", "wo"], [bname],
                               start=False, stop=(c == 3), acc=True)
                        act(xn[0:P, 0:512], bank[0:P, :], AF.Square, [bname], ["xn", "col"], accum_out=col[0:P, 6 + hf:7 + hf])
                    tt("dve", col[0:P, 6:7], col[0:P, 6:7], col[0:P, 7:8], ALU.add, ["col"], ["col"])
                    act(col[0:P, 6:7], col[0:P, 6:7], AF.Ln, ["col"], ["col"], scale=1.0 / D, bias=EPS)
                    act(col[0:P, 7:8], col[0:P, 6:7], AF.Exp, ["col"], ["col"], scale=-0.5)
                    for hf in range(2):
                        bank = psA if hf == 0 else psB
                        bname = "psA" if hf == 0 else "psB"
                        stt(to[0:P, hf * 512:(hf + 1) * 512], bank[0:P, :], col[0:P, 7:8], g_post[0:P, hf * 512:(hf + 1) * 512], ALU.mult, ALU.mult,
                            [bname, "col", "gbc"], [ton])
                    tt("dve", to[0:P, 0:512], to[0:P, 0:512], xb[0:P, 0:512], ALU.add, [ton, xbn], [ton])
                    tt("pool", to[0:P, 512:1024], to[0:P, 512:1024], xb[0:P, 512:1024], ALU.add, [ton, xbn], [ton])
                    dma("sp", S["y"][r0:r0 + P, :], to[0:P, :], [ton], [("ydram", sq_i, ti)], ton + "_o")
                    gti += 1
                    chk(71)

                chk(72)
                dma("sp", S["ss"], hT[:].rearrange("n h p -> n (h p)"), ["hT"], [], "hT_o")
                dma("sp", S["sc"], xbcT[:, :, 0:3], ["xbcT"], [], "xbcT_o")
                chk(73)

            chk(8)
            _st = _STOPPED[0]; _STOPPED[0] = False
            pr.barrier()
            pr.emit()
            _STOPPED[0] = _st

        p2 = ExitStack()
        with p2:
            def sb2(name, shape, dt=F32):
                return p2.enter_context(nc.sbuf_tensor(name, list(shape), dt))

            gbc2 = sb2("gbc2", [128, 2048])
            g_fpre = gbc2[:, 0:1024]; g_fpost = gbc2[:, 1024:2048]
            dma("sp", gbc2[:].unsqueeze(1), gvec[:, 3072:5120].partition_broadcast(128), (), ["gbc"], "gbc2")
            wup = sb2("wup", [128, 8, 2 * DFF], BF16)
            wdn = sb2("wdn", [128, NFF, D], BF16)
            for gi_, j0_ in enumerate(range(0, NFF, 4)):
                nj_ = min(4, NFF - j0_)
                for q4 in range(2):
                    c0_ = q4 * DFF + j0_ * 128
                    dma("pool", wup[:, :, c0_:c0_ + nj_ * 128], w_up[:, c0_:c0_ + nj_ * 128].rearrange("(c p) n -> p c n", p=128), (),
                        [("wup", gi_)], "wup%d" % gi_, max_dma_last_dim=4096)
            for j in range(NFF):
                dma("pool", wdn[:, j, :], w_dn[j * 128:(j + 1) * 128, :], (), ["wdn"], "wdn", max_dma_last_dim=4096)
            hblk = [sb2("hblk0", [128, D]), sb2("hblk1", [128, D]), sb2("hblk2", [128, D])]
            junk2 = sb2("junk2", [128, D], BF16)
            hn = sb2("hn", [128, D], BF16)
            hnT = [sb2("hnT0", [128, 8, 128], BF16), sb2("hnT1", [128, 8, 128], BF16)]
            col2 = sb2("col2", [128, 8]); colp = sb2("colp", [128, 4])
            fhist = sb2("fhist", [128, NFF, 2])
            gs = [sb2("gs0", [128, 4, 130]), sb2("gs1", [128, 4, 130])]
            acc_t = [sb2("acc0", [128, 4, 128]), sb2("acc1", [128, 4, 128])]
            actT = [sb2("actT0", [128, NFF, 128], BF16), sb2("actT1", [128, NFF, 128], BF16)]
            yo = [sb2("yo0", [128, D]), sb2("yo1", [128, D])]

            tiles2 = []
            for sq_i, S in enumerate(seqs):
                P_ = min(128, S["n"])
                for ti in range(S["n"] // P_):
                    tiles2.append((sq_i, S, ti, P_))
            NT2 = len(tiles2)

            def emit_hload(g):
                if g >= NT2:
                    return
                sq_i, S, ti, P = tiles2[g]
                r0 = S["row0"] + ti * P
                hbn = "hblk%d" % (g % 3)
                dma("sp", hblk[g % 3][0:P, :], S["y"][r0:r0 + P, :], [("ydram", sq_i, ti)], [hbn], hbn)

            def prologue(g):
                if g >= NT2:
                    return
                sq_i, S, ti, P = tiles2[g]
                hb = hblk[g % 3]; hbn = "hblk%d" % (g % 3)
                hT_ = hnT[g % 2]; hTn = "hnT%d" % (g % 2)
                act(junk2[0:P, :], hb[0:P, :], AF.Square, [hbn], ["junk2", "colp"], accum_out=colp[0:P, 0:1])
                act(colp[0:P, 1:2], colp[0:P, 0:1], AF.Ln, ["colp"], ["colp"], scale=1.0 / D, bias=EPS)
                act(colp[0:P, 2:3], colp[0:P, 1:2], AF.Exp, ["colp"], ["colp"], scale=-0.5)
                stt(hn[0:P, :], hb[0:P, :], colp[0:P, 2:3], g_fpre[0:P, :], ALU.mult, ALU.mult, [hbn, "colp", "gbc"], ["hn"])
                for dc in range(8):
                    tr(psT[:, dc * 128:dc * 128 + P], hn[0:P, dc * 128:(dc + 1) * 128], ident_b[0:P, 0:P], ["hn", "cst_b"], ["psT"], acc=(dc > 0))
                cp("act", hT_[:, :, 0:P], psT[:, :].rearrange("p (c t) -> p c t", c=8)[:, :, 0:P], ["psT"], [hTn])

            def up_proj(g):
                if g >= NT2:
                    return
                sq_i, S, ti, P = tiles2[g]
                nt = S["n"] // P
                hT_ = hnT[g % 2]; hTn = "hnT%d" % (g % 2)
                aT = actT[g % 2]
                if ti == 0:
                    if S["idx"] is None:
                        mset("dve", fhist[:], 0.0, ["fhist"])
                    else:
                        dma("sp", fhist[:], sfc[S["idx"]], (), ["fhist"], "fhist")
                groups = [(j0, min(4, NFF - j0)) for j0 in range(0, NFF, 4)]

                def stPA(gi):
                    j0, nj = groups[gi]
                    i2 = gi % 2
                    gb = psS[i2]; gbn = "psS%d" % i2
                    ub = psL[i2]; ubn = "psL%d" % i2
                    for jj in range(nj):
                        j = j0 + jj
                        for dc in range(8):
                            mm(gb[:, jj * 128:jj * 128 + P], wup[:, dc, j * 128:(j + 1) * 128], hT_[:, dc, 0:P], [("wup", gi), hTn], [gbn],
                               start=(dc == 0), stop=(dc == 7), acc=(dc > 0 or jj > 0))
                    for jj in range(nj):
                        j = j0 + jj
                        for dc in range(8):
                            mm(ub[:, jj * 128:jj * 128 + P], wup[:, dc, DFF + j * 128:DFF + (j + 1) * 128], hT_[:, dc, 0:P], [("wup", gi), hTn], [ubn],
                               start=(dc == 0), stop=(dc == 7), acc=(dc > 0 or jj > 0))
                    g_ = gs[i2]; gn = "gs%d" % i2
                    a_ = acc_t[i2]; an = "acc%d" % i2
                    gview = gb[:, :].rearrange("p (j t) -> p j t", j=4)[:, 0:nj, 0:P]
                    cp("dve", g_[:, 0:nj, 0:2], fhist[:, j0:j0 + nj, :], ["fhist"], [gn])
                    cp("act", g_[:, 0:nj, 2:2 + P], gview, [gbn], [gn])
                    cp("dve", fhist[:, j0:j0 + nj, :], g_[:, 0:nj, P:P + 2], [gn], ["fhist"])
                    for jj in range(nj):
                        j = j0 + jj
                        act(a_[:, jj, 0:P], gb[:, jj * 128:jj * 128 + P], AF.Identity, [gbn, "wc_f"], [(an, jj)],
                            scale=wc_f[:, j, 2:3], bias=wc_f[:, j, 3:4])
                    for tap in (1, 0):
                        for jj in range(nj):
                            j = j0 + jj
                            stt(a_[:, jj, 0:P], g_[:, jj, tap:tap + P], wc_f[:, j, tap:tap + 1], a_[:, jj, 0:P], ALU.mult, ALU.add,
                                [gn, "wc_f", (an, jj)], [(an, jj)])

                def stB(gi):
                    j0, nj = groups[gi]
                    i2 = gi % 2
                    ub = psL[i2]; ubn = "psL%d" % i2
                    a_ = acc_t[i2]; an = "acc%d" % i2
                    uview = ub[:, :].rearrange("p (j t) -> p j t", j=4)[:, 0:nj, 0:P]
                    a_all = [(an, jj) for jj in range(4)]
                    act(a_[:, 0:nj, 0:P], a_[:, 0:nj, 0:P], AF.Gelu_apprx_tanh, a_all, a_all)
                    tt("dve", aT[:, j0:j0 + nj, 0:P], a_[:, 0:nj, 0:P], uview, ALU.mult, a_all + [ubn], [("actT", g % 2, gi)])

                for gi in range(len(groups) + 1):
                    if gi < len(groups):
                        stPA(gi)
                    if gi >= 1:
                        stB(gi - 1)
                if ti == nt - 1:
                    dma("sp", S["fc"], fhist[:], ["fhist"], [], "fhist_o")

            def down_proj(g):
                sq_i, S, ti, P = tiles2[g]
                r0 = S["row0"] + ti * P
                hb = hblk[g % 3]; hbn = "hblk%d" % (g % 3)
                aT = actT[g % 2]
                yo_ = yo[g % 2]; yon = "yo%d" % (g % 2)
                for hf in range(2):
                    bank = psA if hf == 0 else psB
                    bname = "psA" if hf == 0 else "psB"
                    for j in range(NFF):
                        mm(bank[0:P, :], aT[:, j, 0:P], wdn[:, j, hf * 512:(hf + 1) * 512], [("actT", g % 2, j // 4), "wdn"], [bname],
                           start=(j == 0), stop=(j == NFF - 1), acc=(j > 0))
                    act(junk2[0:P, 0:512], bank[0:P, :], AF.Square, [bname], ["junk2", "col2"], accum_out=col2[0:P, 3 + hf:4 + hf])
                tt("dve", col2[0:P, 3:4], col2[0:P, 3:4], col2[0:P, 4:5], ALU.add, ["col2"], ["col2"])
                act(col2[0:P, 5:6], col2[0:P, 3:4], AF.Ln, ["col2"], ["col2"], scale=1.0 / D, bias=EPS)
                act(col2[0:P, 6:7], col2[0:P, 5:6], AF.Exp, ["col2"], ["col2"], scale=-0.5)
                for hf in range(2):
                    bank = psA if hf == 0 else psB
                    bname = "psA" if hf == 0 else "psB"
                    stt(yo_[0:P, hf * 512:(hf + 1) * 512], bank[0:P, :], col2[0:P, 6:7], g_fpost[0:P, hf * 512:(hf + 1) * 512], ALU.mult, ALU.mult,
                        [bname, "col2", "gbc"], [(yon, hf)])
                tt("dve", yo_[0:P, 0:512], yo_[0:P, 0:512], hb[0:P, 0:512], ALU.add, [(yon, 0), hbn], [(yon, 0)])
                tt("pool", yo_[0:P, 512:1024], yo_[0:P, 512:1024], hb[0:P, 512:1024], ALU.add, [(yon, 1), hbn], [(yon, 1)])
                dma("sp", S["y"][r0:r0 + P, :], yo_[0:P, :], [(yon, 0), (yon, 1)], [("ydram2", sq_i, ti)], yon + "_o")
                emit_hload(g + 3)

            emit_hload(0)
            emit_hload(1)
            emit_hload(2)
            prologue(0)
            prologue(1)
            up_proj(0)
            for g in range(NT2):
                up_proj(g + 1)
                prologue(g + 2)
                down_proj(g)

            _STOPPED[0] = False
            pr.barrier(final=True)
            pr.emit()
    return nc


def _consts():
    j = np.arange(128)[:, None]; s = np.arange(128)[None, :]
    ident = (j == s).astype(np.float32)
    triincl = (j <= s).astype(np.float32)
    ones = np.ones((128, 128), np.float32)
    negmask = np.where(j <= s, 0.0, -1e5).astype(np.float32)
    pad = np.zeros((128, 128), np.float32)
    cst = np.concatenate([ident, triincl, ones, negmask, pad], axis=1)
    negtri = -(j > s).astype(np.float32)
    negones = -ones
    mask01 = (j < s).astype(np.float32)
    cstb = np.concatenate([ident, negtri, negones, mask01], axis=1)
    return np.ascontiguousarray(cst), np.ascontiguousarray(cstb)


def _chan_layout(a):
    lead = a.shape[:-1]
    out = np.zeros((128, 8) + lead, np.float32)
    for c in range(4):
        out[:, c] = np.moveaxis(a[..., c * 128:(c + 1) * 128], -1, 0)
    for c in range(4):
        out[0:64, 4 + c] = np.moveaxis(a[..., 512 + c * 64:512 + (c + 1) * 64], -1, 0)
    return out


def _chan_unlayout(d):
    k = d.shape[2]
    out = np.zeros((k, 768), np.float32)
    for c in range(4):
        out[:, c * 128:(c + 1) * 128] = d[:, c, :].T
    for c in range(4):
        out[:, 512 + c * 64:512 + (c + 1) * 64] = d[0:64, 4 + c, :].T
    return out


def make_in_maps(inp, n_cores, NP, PAST, NS, NSEQ):
    cst, cstb = _consts()
    f = lambda a: np.ascontiguousarray(np.asarray(a, dtype=np.float32))
    gvec = np.concatenate([inp["g_mix_pre"][0], inp["g_mix_post"][0], inp["g_ssm_out"][0], inp["g_attn_out"][0], inp["g_ffn_pre"][0], inp["g_ffn_post"][0]])[None, :]
    hvec = np.concatenate([inp["dt_bias"][0], inp["a_log"][0], inp["d_skip"][0]])[None, :]
    wcs = _chan_layout(np.concatenate([inp["ssm_conv_w"][0], inp["ssm_conv_b"][0][None, :]], axis=0))
    wcf_src = np.concatenate([inp["ffn_conv_w"][0], inp["ffn_conv_b"][0][None, :]], axis=0)
    wcf = wcf_src.reshape(4, NFF, 128).transpose(2, 1, 0)
    shared = dict(w_in=f(inp["w_in"][0]), w_out=f(inp["w_out"][0]), w_up=f(inp["w_up"][0]), w_dn=f(inp["w_down"][0]),
                  gvec=f(gvec), hvec=f(hvec), wcs=f(wcs), wcf=f(wcf), cst=cst, cstb=cstb)
    maps = []
    for c in range(n_cores):
        bs = list(range(c * NSEQ, (c + 1) * NSEQ))
        m = dict(shared)
        m["xp"] = f(inp["x_prompt"][c])
        m["xs"] = f(inp["x_sample"][bs].reshape(NSEQ * NS, D))
        m["ck"] = f(inp["cache_k"][0, bs].reshape(NSEQ, PAST, DATT))
        m["cv"] = f(inp["cache_v"][0, bs].reshape(NSEQ, PAST, DATT))
        m["sst"] = f(inp["state_ssm"][0, bs].transpose(0, 3, 1, 2).reshape(NSEQ, NST, DSSM))
        m["scv"] = f(np.stack([_chan_layout(inp["state_ssm_conv"][0, b]) for b in bs]))
        m["sfc"] = f(np.stack([inp["state_ffn_conv"][0, b].reshape(2, NFF, 128).transpose(2, 1, 0) for b in bs]))
        maps.append(m)
    return maps


def gather(results, n_cores, NP, PAST, NS, NSEQ):
    B = n_cores; BS = n_cores * NSEQ
    y_p = np.stack([r["yp"] for r in results])
    y_s = np.concatenate([r["ys"].reshape(NSEQ, NS, D) for r in results])
    k_p = np.stack([r["kTp"].transpose(2, 0, 1) for r in results])[None]
    v_p = np.stack([r["vp"].reshape(NP, NH, HD) for r in results])[None]
    ssm_p = np.stack([r["ssp"].reshape(NST, NH, HD).transpose(1, 2, 0) for r in results])[None]
    sc_p = np.stack([_chan_unlayout(r["scp"]) for r in results])[None]
    fc_p = np.stack([r["fcp"].transpose(2, 1, 0).reshape(2, DFF) for r in results])[None]
    k_s = np.concatenate([r["kTs"].transpose(0, 3, 1, 2) for r in results])[None]
    v_s = np.concatenate([r["vs"].reshape(NSEQ, NS, NH, HD) for r in results])[None]
    ssm_s = np.concatenate([r["sss"].reshape(NSEQ, NST, NH, HD).transpose(0, 2, 3, 1) for r in results])[None]
    sc_s = np.concatenate([np.stack([_chan_unlayout(r["scs"][b]) for b in range(NSEQ)]) for r in results])[None]
    fc_s = np.concatenate([np.stack([r["fcs"][b].transpose(2, 1, 0).reshape(2, DFF) for b in range(NSEQ)]) for r in results])[None]
    outs = (y_p, y_s, k_p, v_p, ssm_p, sc_p, fc_p, k_s, v_s, ssm_s, sc_s, fc_s)
    return tuple(np.ascontiguousarray(o, dtype=np.float32) for o in outs)


def kernel(**inputs):
    inp = {k: np.asarray(v) for k, v in inputs.items()}
    n_cores = 8
    NP = inp["x_prompt"].shape[1]; PAST = inp["cache_k"].shape[2]; NS = inp["x_sample"].shape[1]
    NSEQ = inp["x_sample"].shape[0] // n_cores
    nc = build(NP, PAST, NS, NSEQ)
    maps = make_in_maps(inp, n_cores, NP, PAST, NS, NSEQ)
    res = run_bass_kernel_spmd(nc, maps, core_ids=list(range(n_cores)))
    return gather(res.results, n_cores, NP, PAST, NS, NSEQ)
```

```python
import numpy as np
import ml_dtypes
from contextlib import ExitStack
import concourse.bass as bass
import concourse.mybir as mybir
from concourse.bass_utils import run_bass_kernel_spmd

F32 = mybir.dt.float32
BF16 = mybir.dt.bfloat16
AF = mybir.ActivationFunctionType
ALU = mybir.AluOpType

D = 1024
DATT = 512
NH = 8
HD = 64
DSSM = 512
NST = 64
CONV_DIM = 768
DFF = 2816
NFF = DFF // 128
DIN = 2824
EPS = 1e-6
ENGS = ["pe", "act", "dve", "pool", "sp"]
import os as _os
STOP = int(_os.environ.get("KSTOP", "99"))


_STOPPED = [False]


_GTI = [0]


def chk(k):
    if STOP == k:
        _STOPPED[0] = True
    if STOP == 1000 + k and _GTI[0] == 1:
        _STOPPED[0] = True


class Prog:
    def __init__(self, nc, stack):
        self.nc = nc
        self.stack = stack
        self.ops = {e: [] for e in ENGS}
        self.esem = {e: stack.enter_context(nc.semaphore("s_" + e)) for e in ENGS}
        self.dsem = {}
        self.lastw = {}
        self.readers = {}
        self.known = {e: {} for e in ENGS}
        self.emitted = {e: 0 for e in ENGS}
        self.sigbase = {e: 0 for e in ENGS}
        self.eng_obj = {"pe": nc.tensor, "act": nc.scalar, "dve": nc.vector, "pool": nc.gpsimd, "sp": nc.sync}

    def _need(self, eng, tok, waits):
        if tok is None:
            return
        if tok[0] == "e":
            _, x, idx = tok
            if self.known[eng].get(("e", x), 0) >= idx:
                return
            self.known[eng][("e", x)] = idx
            self.ops[x][idx - 1]["signal"] = True
        else:
            _, key, val = tok
            if self.known[eng].get(("d", key), 0) >= val:
                return
            self.known[eng][("d", key)] = val
        waits.append(tok)

    def op(self, eng, fn, reads=(), writes=(), dma_key=None, pe_acc=False):
        if _STOPPED[0]:
            return None
        waits = []
        for r in reads:
            self._need(eng, self.lastw.get(r), waits)
        for w in writes:
            lw = self.lastw.get(w)
            if not (pe_acc and lw is not None and lw[0] == "e" and lw[1] == "pe" and eng == "pe"):
                self._need(eng, lw, waits)
            for t in self.readers.get(w, ()):
                self._need(eng, t, waits)
        idx = len(self.ops[eng]) + 1
        rec = {"fn": fn, "waits": waits, "signal": False, "dma": None}
        if dma_key is not None:
            if dma_key not in self.dsem:
                self.dsem[dma_key] = [self.stack.enter_context(self.nc.semaphore("d%d" % len(self.dsem))), 0]
            self.dsem[dma_key][1] += 16
            rec["dma"] = dma_key
            tok = ("d", dma_key, self.dsem[dma_key][1])
        else:
            tok = ("e", eng, idx)
        self.ops[eng].append(rec)
        for w in writes:
            self.lastw[w] = tok
            self.readers[w] = []
        for r in reads:
            if r not in writes:
                self.readers.setdefault(r, []).append(tok)
        return tok

    def barrier(self, final=False):
        lasts = {}
        for e in ENGS:
            k = len(self.ops[e])
            while k > 0 and (self.ops[e][k - 1]["fn"] is None or self.ops[e][k - 1]["dma"] is not None):
                k -= 1
            lasts[e] = k
        dvals = {k: v[1] for k, v in self.dsem.items()}
        for e in (["sp"] if final else ENGS):
            waits = []
            for x in ENGS:
                if x != e and lasts[x] > 0:
                    self._need(e, ("e", x, lasts[x]), waits)
            for k, v in dvals.items():
                if v > 0:
                    self._need(e, ("d", k, v), waits)
            self.ops[e].append({"fn": None, "waits": waits, "signal": False, "dma": None})
        if not final:
            self.lastw.clear()
            self.readers.clear()

    def emit(self):
        pref = {}
        for e in ENGS:
            c = 0
            arr = []
            for o in self.ops[e]:
                if o["signal"] and o["dma"] is None and o["fn"] is not None:
                    c += 1
                arr.append(c)
            pref[e] = arr
        start = dict(self.emitted)

        def run(e, engine):
            ops = self.ops[e]
            for i in range(start[e], len(ops)):
                o = ops[i]
                for t in o["waits"]:
                    if t[0] == "e":
                        engine.wait_ge(self.esem[t[1]], pref[t[1]][t[2] - 1])
                    else:
                        engine.wait_ge(self.dsem[t[1]][0], t[2])
                if o["fn"] is None:
                    continue
                ins = o["fn"](engine)
                if o["dma"] is not None:
                    ins.then_inc(self.dsem[o["dma"]][0], 16)
                elif o["signal"]:
                    ins.then_inc(self.esem[e], 1)

        with self.nc.Block() as block:
            @block.tensor
            def _(t):
                run("pe", t)

            @block.scalar
            def _(a):
                run("act", a)

            @block.vector
            def _(v):
                run("dve", v)

            @block.gpsimd
            def _(g):
                run("pool", g)

            @block.sync
            def _(s):
                run("sp", s)
        for e in ENGS:
            self.emitted[e] = len(self.ops[e])


def build(NP=2048, PAST=2048, NS=32, NSEQ=2):
    nc = bass.Bass("TRN2", target_bir_lowering=False)
    KMAX = max(NP, PAST + NS)
    NKT = (KMAX + 127) // 128
    st = ExitStack()
    with st:
        def din(name, shape, dt=F32):
            return nc.dram_tensor(name, list(shape), dt, kind="ExternalInput").ap()

        def dout(name, shape):
            return nc.dram_tensor(name, list(shape), F32, kind="ExternalOutput").ap()

        xp = din("xp", [NP, D]); xs = din("xs", [NSEQ * NS, D])
        ck = din("ck", [NSEQ, PAST, DATT]); cv = din("cv", [NSEQ, PAST, DATT])
        sst = din("sst", [NSEQ, NST, DSSM]); scv = din("scv", [NSEQ, 128, 8, 3]); sfc = din("sfc", [NSEQ, 128, NFF, 2])
        w_in = din("w_in", [D, DIN]); w_out = din("w_out", [D, D]); w_up = din("w_up", [D, 2 * DFF]); w_dn = din("w_dn", [DFF, D])
        gvec = din("gvec", [1, 5120]); hvec = din("hvec", [1, 24])
        wcs = din("wcs", [128, 8, 5]); wcf = din("wcf", [128, NFF, 4]); cst = din("cst", [128, 5 * 128])
        yp = dout("yp", [NP, D]); ys = dout("ys", [NSEQ * NS, D])
        kTp = dout("kTp", [NH, HD, NP]); vp = dout("vp", [NP, DATT])
        kTs = dout("kTs", [NSEQ, NH, HD, NS]); vs = dout("vs", [NSEQ * NS, DATT])
        ssp = dout("ssp", [NST, DSSM]); sss = dout("sss", [NSEQ, NST, DSSM])
        scp = dout("scp", [128, 8, 3]); scs = dout("scs", [NSEQ, 128, 8, 3])
        fcp = dout("fcp", [128, NFF, 2]); fcs = dout("fcs", [NSEQ, 128, NFF, 2])

        pr = Prog(nc, st)

        def sb(name, shape, dt=F32):
            return st.enter_context(nc.sbuf_tensor(name, list(shape), dt))

        def ps(name, shape, dt=F32):
            return st.enter_context(nc.psum_tensor(name, list(shape), dt))

        def mm(out, lhsT, rhs, reads, writes, start=True, stop=True, acc=False, **kw):
            pr.op("pe", lambda e: e.matmul(out, lhsT=lhsT, rhs=rhs, start=start, stop=stop, **kw), reads, writes, pe_acc=acc)

        def tr(out, in_, ident, reads, writes, acc=False):
            pr.op("pe", lambda e: e.transpose(out, in_, ident), reads, writes, pe_acc=acc)

        def act(out, in_, func, reads, writes, **kw):
            pr.op("act", lambda e: e.activation(out=out, in_=in_, func=func, **kw), reads, writes)

        def tt(eng, out, in0, in1, op, reads, writes):
            pr.op(eng, lambda e: e.tensor_tensor(out=out, in0=in0, in1=in1, op=op), reads, writes)

        def ts(eng, out, in0, s1, s2, op0, op1, reads, writes):
            if s2 is None:
                pr.op(eng, lambda e: e.tensor_scalar(out=out, in0=in0, scalar1=s1, scalar2=None, op0=op0), reads, writes)
            else:
                pr.op(eng, lambda e: e.tensor_scalar(out=out, in0=in0, scalar1=s1, scalar2=s2, op0=op0, op1=op1), reads, writes)

        def stt(out, in0, scalar, in1, op0, op1, reads, writes):
            pr.op("dve", lambda e: e.scalar_tensor_tensor(out=out, in0=in0, scalar=scalar, in1=in1, op0=op0, op1=op1), reads, writes)

        def cp(eng, out, in_, reads, writes):
            if eng == "act":
                pr.op("act", lambda e: e.copy(out=out, in_=in_), reads, writes)
            else:
                pr.op(eng, lambda e: e.tensor_copy(out=out, in_=in_), reads, writes)

        def mset(eng, ap, val, writes):
            pr.op(eng, lambda e: e.memset(ap, val), (), writes)

        def dma(q, out, in_, reads, writes, key, **kw):
            pr.op(q, lambda e: e.dma_start(out=out, in_=in_, **kw), reads, writes, dma_key=key)

        cst_f = sb("cst_f", [128, 5 * 128])
        ident_f = cst_f[:, 0:128]; triincl = cst_f[:, 128:256]; ones_f = cst_f[:, 256:384]; negmask = cst_f[:, 384:512]
        cst_b = sb("cst_b", [128, 4 * 128], BF16)
        ident_b = cst_b[:, 0:128]; negtri = cst_b[:, 128:256]; negones = cst_b[:, 256:384]; mask01 = cst_b[:, 384:512]
        cstb_d = din("cstb", [128, 4 * 128])
        hv = sb("hv", [128, 24])
        a_bc = sb("a_bc", [128, 8]); d_bc = sb("d_bc", [128, 512])
        wc_s = sb("wc_s", [128, 8, 5]); wc_f = sb("wc_f", [128, NFF, 4])

        dma("sp", cst_f[:], cst[:, :], (), ["cst_f"], "cst_f")
        dma("pool", cst_b[:], cstb_d[:, :], (), ["cst_b"], "cst_b")
        dma("sp", hv[:].unsqueeze(1), hvec.partition_broadcast(128), (), ["hv"], "hv")
        dma("sp", wc_s[:], wcs[:, :, :], (), ["wc_s"], "wc_s")
        dma("sp", wc_f[:], wcf[:, :, :], (), ["wc_f"], "wc_f")
        act(a_bc[:], hv[:, 8:16], AF.Exp, ["hv"], ["a_bc"])
        ts("dve", a_bc[:], a_bc[:], -1.0, None, ALU.mult, None, ["a_bc"], ["a_bc"])
        cp("dve", d_bc[:].rearrange("p (h d) -> p h d", h=8), hv[:, 16:24].unsqueeze(2).to_broadcast([128, 8, 64]), ["hv"], ["d_bc"])

        psA = ps("psA", [128, 512]); psB = ps("psB", [128, 512])
        psS = [ps("psS0", [128, 512]), ps("psS1", [128, 512])]
        psL = [ps("psL0", [128, 512]), ps("psL1", [128, 512])]
        psO = ps("psO", [128, 512])
        psT = ps("psT", [128, 1024], BF16)

        seqs = [dict(n=NP, past=0, x=xp, y=yp, row0=0, kT=kTp, v=vp, ss=ssp[:, :], sc=scp[:, :, :], fc=fcp[:, :, :], idx=None)]
        for b in range(NSEQ):
            seqs.append(dict(n=NS, past=PAST, x=xs, y=ys, row0=b * NS, kT=kTs[b], v=vs, ss=sss[b], sc=scs[b], fc=fcs[b], idx=b))

        p1 = ExitStack()
        with p1:
            def sb1(name, shape, dt=F32):
                return p1.enter_context(nc.sbuf_tensor(name, list(shape), dt))

            gbc = sb1("gbc", [128, 3072])
            g_pre = gbc[:, 0:1024]; g_post = gbc[:, 1024:2048]; g_ssm = gbc[:, 2048:2560]; g_attn = gbc[:, 2560:3072]
            dma("sp", gbc[:].unsqueeze(1), gvec[:, 0:3072].partition_broadcast(128), (), ["gbc"], "gbc")
            win = sb1("win", [128, 8, DIN], BF16)
            wo = sb1("wo", [128, 8, D], BF16)
            for bi_, ca_, cb_ in [(2, 2048, DIN), (0, 0, 1024), (1, 1024, 2048)]:
                dma("pool", win[:, :, ca_:cb_], w_in[:, ca_:cb_].rearrange("(c p) n -> p c n", p=128), (), [("win", bi_)], "win%d" % bi_, max_dma_last_dim=4096)
            for c in range(8):
                dma("pool", wo[:, c, :], w_out[c * 128:(c + 1) * 128, :], (), ["wo"], "wo", max_dma_last_dim=4096)

            KT = sb1("KT", [64, 8, NKT * 128], BF16)
            V = sb1("V", [128, NKT, 512], BF16)
            xblk = [sb1("xblk0", [128, D]), sb1("xblk1", [128, D])]
            xn = sb1("xn", [128, D], BF16)
            xnT = sb1("xnT", [128, 8, 128], BF16)
            qT = sb1("qT", [64, 8, 128], BF16)
            kst = sb1("kst", [64, 8, 128])
            vst = sb1("vst", [128, 512])
            xbcT = sb1("xbcT", [128, 8, 131])
            xcT = sb1("xcT", [128, 8, 128])
            BCTb = sb1("BCTb", [64, 4, 128], BF16)
            col = sb1("col", [128, 12])
            dtt = sb1("dtt", [128, 8]); dA = sb1("dA", [128, 8]); cum = sb1("cum", [128, 8]); te = sb1("te", [128, 8])
            e_t = [sb1("e0", [128, 512]), sb1("e1", [128, 512])]
            sp_t = [sb1("sp0", [128, 512]), sb1("sp1", [128, 512])]
            spb_t = [sb1("spb0", [128, 512], BF16), sb1("spb1", [128, 512], BF16)]
            w_t = [sb1("w0", [128, 512], BF16), sb1("w1", [128, 512], BF16)]
            Racc = [sb1("Racc0", [128, 512], BF16), sb1("Racc1", [128, 512], BF16)]
            zb = sb1("zb", [128, 512], BF16)
            attn_n = sb1("attn_n", [128, 512], BF16)
            attn_nT = sb1("attn_nT", [128, 4, 128], BF16)
            segb = sb1("segb", [128, 8, 128])
            MT = sb1("MT", [128, 8, 128], BF16)
            E_bc = sb1("E_bc", [64, 8, 128])
            CsT = sb1("CsT", [64, 8, 128], BF16)
            x_tok = sb1("x_tok", [128, 512]); xdt = sb1("xdt", [128, 512], BF16); xw = sb1("xw", [128, 512], BF16)
            B_tok = sb1("B_tok", [128, 128], BF16)
            hT = sb1("hT", [64, 8, 64]); hTb = sb1("hTb", [64, 8, 64], BF16); htmp = sb1("htmp", [64, 8, 64])
            sz = sb1("sz", [128, 512]); yf = sb1("yf", [128, 512]); yn = sb1("yn", [128, 512], BF16)
            ynT = sb1("ynT", [128, 4, 128], BF16)
            tmpo = [sb1("tmpo0", [128, D])] * 2
            kcs = vst; kcb = xdt

            if _os.environ.get("KDEBUG"):
                print("phase1 sbuf remaining", nc.sbuf_bytes_remaining)
            mset("dve", xbcT[:], 0.0, ["xbcT"])
            mset("dve", xcT[:], 0.0, [("xcT", c) for c in range(8)])
            mset("pool", KT[:], 0.0, [("KT", i) for i in range(NKT)])
            mset("pool", zb[:], 0.0, ["zb"])

            gti = 0
            chk(0)
            all_tiles = []
            for S_ in seqs:
                P_ = min(128, S_["n"])
                for ti_ in range(S_["n"] // P_):
                    all_tiles.append((S_, ti_, P_))

            def emit_xload(g):
                if g >= len(all_tiles):
                    return
                S_, ti_, P_ = all_tiles[g]
                r0_ = S_["row0"] + ti_ * P_
                dma("sp", xblk[g % 2][0:P_, :], S_["x"][r0_:r0_ + P_, :], (), ["xblk%d" % (g % 2)], "xblk%d" % (g % 2))
            for sq_i, S in enumerate(seqs):
                n = S["n"]; past = S["past"]
                P = min(128, n); nt = n // P
                if S["idx"] is None:
                    mset("dve", hT[:], 0.0, ["hT"])
                    mset("pool", hTb[:], 0.0, ["hTb"])
                    mset("dve", xbcT[:, :, 0:3], 0.0, ["xbcT"])
                else:
                    b = S["idx"]
                    dma("sp", hT[:].rearrange("n h p -> n (h p)"), sst[b], (), ["hT"], "hT")
                    cp("pool", hTb[:], hT[:], ["hT"], ["hTb"])
                    dma("sp", xbcT[:, :, 0:3], scv[b], (), ["xbcT"], "xbcT")
                    for kt in range(past // 128):
                        kb_, kn_ = [(vst, "vst"), (x_tok, "x_tok")][kt % 2]
                        vb_, vn_ = [(yf, "yf"), (sz, "sz")][kt % 2]
                        dma("sp", kb_[:], ck[b, kt * 128:(kt + 1) * 128, :], (), [kn_], kn_)
                        dma("sp", vb_[:], cv[b, kt * 128:(kt + 1) * 128, :], (), [vn_], vn_)
                        cp("dve", kcb[:], kb_[:], [kn_], ["xdt"])
                        for h in range(8):
                            tr(psT[0:64, h * 128:(h + 1) * 128], kcb[:, h * 64:(h + 1) * 64], ident_b, ["xdt", "cst_b"], ["psT"], acc=(h > 0))
                        cp("act", KT[:, :, kt * 128:(kt + 1) * 128], psT[0:64, :].rearrange("p (h t) -> p h t", h=8), ["psT"], [("KT", kt)])
                        cp("dve", V[:, kt, :], vb_[:], [vn_], [("V", kt)])

                for ti in range(nt):
                    t0 = ti * P
                    kp0 = past + t0
                    _GTI[0] = gti
                    ktile = kp0 // 128
                    krow = kp0 % 128
                    assert krow == 0
                    xb = xblk[gti % 2]; xbn = "xblk%d" % (gti % 2)
                    r0 = S["row0"] + t0
                    if gti == 0:
                        emit_xload(0)
                    emit_xload(gti + 1)
                    chk(1)
                    act(xn[0:P, :], xb[0:P, :], AF.Square, [xbn], ["xn", "col"], accum_out=col[0:P, 0:1])
                    act(col[0:P, 1:2], col[0:P, 0:1], AF.Ln, ["col"], ["col"], scale=1.0 / D, bias=EPS)
                    act(col[0:P, 2:3], col[0:P, 1:2], AF.Exp, ["col"], ["col"], scale=-0.5)
                    stt(xn[0:P, :], xb[0:P, :], col[0:P, 2:3], g_pre[0:P, :], ALU.mult, ALU.mult, [xbn, "col", "gbc"], ["xn"])
                    chk(2)
                    for dc in range(8):
                        tr(psT[:, dc * 128:dc * 128 + P], xn[0:P, dc * 128:(dc + 1) * 128], ident_b[0:P, 0:P], ["xn", "cst_b"], ["psT"], acc=(dc > 0))
                    cp("act", xnT[:, :, 0:P], psT[:, :].rearrange("p (c t) -> p c t", c=8)[:, :, 0:P], ["psT"], ["xnT"])
                    chk(3)
                    chk(34)
                    for c in range(4):
                        c0 = 2048 + c * 128
                        for dc in range(8):
                            mm(psA[:, c * 128:c * 128 + P], win[:, dc, c0:c0 + 128], xnT[:, dc, 0:P], [("win", 2), "xnT"], ["psA"],
                               start=(dc == 0), stop=(dc == 7), acc=(dc > 0 or c > 0))
                    cp("act", xbcT[:, 0:4, 3:3 + P], psA[:, :].rearrange("p (c t) -> p c t", c=4)[:, :, 0:P], ["psA"], ["xbcT"])
                    for c in range(4):
                        c0 = 2560 + c * 64
                        for dc in range(8):
                            mm(psB[0:64, c * 128:c * 128 + P], win[:, dc, c0:c0 + 64], xnT[:, dc, 0:P], [("win", 2), "xnT"], ["psB"],
                               start=(dc == 0), stop=(dc == 7), acc=(dc > 0 or c > 0))
                    cp("act", xbcT[0:64, 4:8, 3:3 + P], psB[0:64, :].rearrange("p (c t) -> p c t", c=4)[:, :, 0:P], ["psB"], ["xbcT"])
                    chk(35)
                    for dc in range(8):
                        mm(psL[0][0:P, 0:8], xnT[:, dc, 0:P], win[:, dc, 2816:2824], [("win", 2), "xnT"], ["psL0"], start=(dc == 0), stop=(dc == 7), acc=(dc > 0))
                    tt("dve", dtt[0:P, :], psL[0][0:P, 0:8], hv[0:P, 0:8], ALU.add, ["psL0", "hv"], ["dtt"])
                    act(dtt[0:P, :], dtt[0:P, :], AF.Exp, ["dtt"], ["dtt"])
                    act(dtt[0:P, :], dtt[0:P, :], AF.Ln, ["dtt"], ["dtt"], bias=1.0)
                    tt("dve", dA[0:P, :], dtt[0:P, :], a_bc[0:P, :], ALU.mult, ["dtt", "a_bc"], ["dA"])

                    chk(4)
                    for c in range(8):
                        M = 128 if c < 4 else 64
                        ts("dve", xcT[0:M, c, 0:P], xbcT[0:M, c, 3:3 + P], wc_s[0:M, c, 3:4], wc_s[0:M, c, 4:5], ALU.mult, ALU.add,
                           ["xbcT", "wc_s"], [("xcT", c)])
                    for i in range(3):
                        for c in range(8):
                            M = 128 if c < 4 else 64
                            stt(xcT[0:M, c, 0:P], xbcT[0:M, c, i:i + P], wc_s[0:M, c, i:i + 1], xcT[0:M, c, 0:P], ALU.mult, ALU.add,
                                ["xbcT", "wc_s", ("xcT", c)], [("xcT", c)])
                    for grp in range(4):
                        bank = psA if grp % 2 == 0 else psB
                        bname = "psA" if grp % 2 == 0 else "psB"
                        for j in range(4):
                            c0 = (grp * 4 + j) * 64
                            for dc in range(8):
                                mm(bank[0:64, j * 128:j * 128 + P], win[:, dc, c0:c0 + 64], xnT[:, dc, 0:P], [("win", 0), "xnT"], [bname],
                                   start=(dc == 0), stop=(dc == 7), acc=(dc > 0 or j > 0))
                        src = bank[0:64, :].rearrange("p (j t) -> p j t", j=4)[:, :, 0:P]
                        if grp < 2:
                            act(qT[:, grp * 4:(grp + 1) * 4, 0:P], src, AF.Copy, [bname], ["qT"], scale=0.125)
                        else:
                            g2 = grp - 2
                            cp("act", KT[:, g2 * 4:(g2 + 1) * 4, kp0:kp0 + P], src, [bname], [("KT", ktile)])
                            cp("act", kst[:, g2 * 4:(g2 + 1) * 4, 0:P], src, [bname], ["kst"])
                    chk(31)
                    dma("sp", S["kT"][:, :, t0:t0 + P].rearrange("h d t -> d h t"), kst[:, :, 0:P], ["kst"], [], "kst_o")
                    chk(32)
                    for dc in range(8):
                        mm(psA[0:P, :], xnT[:, dc, 0:P], win[:, dc, 1024:1536], [("win", 1), "xnT"], ["psA"], start=(dc == 0), stop=(dc == 7), acc=(dc > 0))
                    chk(321)
                    cp("act", V[0:P, ktile, :], psA[0:P, :], ["psA"], [("V", ktile)])
                    chk(322)
                    cp("act", vst[0:P, :], psA[0:P, :], ["psA"], ["vst"])
                    chk(323)
                    dma("sp", S["v"][r0:r0 + P, :], vst[0:P, :], ["vst"], [], "vst_o")
                    chk(33)
                    for dc in range(8):
                        mm(psB[0:P, :], xnT[:, dc, 0:P], win[:, dc, 1536:2048], [("win", 1), "xnT"], ["psB"], start=(dc == 0), stop=(dc == 7), acc=(dc > 0))
                    act(sz[0:P, :], psB[0:P, :], AF.Silu, ["psB"], ["sz"])
                    xc_all = [("xcT", c) for c in range(8)]
                    act(xcT[:, :, 0:P], xcT[:, :, 0:P], AF.Silu, xc_all, xc_all)
                    cp("dve", xbcT[:, :, 0:3], xbcT[:, :, P:P + 3], ["xbcT"], ["xbcT"])
                    cp("dve", BCTb[:, :, 0:P], xcT[0:64, 4:8, 0:P], xc_all, ["BCTb"])

                    chk(5)
                    ktl = []
                    for kt in range(ktile + 1):
                        ktl.append((kt, P if kt == ktile else 128, kt == ktile))
                    ktl = ktl[::-1]
                    W = 4 * P
                    units = []
                    for li, (kt, nk, diag) in enumerate(ktl):
                        for half in range(2):
                            units.append((li, kt, nk, diag, half))
                    for half in range(2):
                        mset("pool", Racc[half][:, 0:W], 0.0, ["Racc%d" % half])
                    mm(psO[0:P, :], zb[:, 0:P], zb[:, :], ["zb"], ["psO"], start=True, stop=False)

                    nkt_ = len(ktl)

                    def tbuf(half):
                        return segb[:, 4 * half:4 * half + 4, :].rearrange("k r t -> k (r t)")

                    def names(half):
                        return ("psS%d" % half, "psL%d" % half, "e%d" % half, "sp%d" % half, "spb%d" % half, "w%d" % half, "Racc%d" % half)

                    def st_S(li, kt, nk, diag, half):
                        Sn = names(half)[0]
                        for j in range(4):
                            h = 4 * half + j
                            mm(psS[half][0:nk, j * P:(j + 1) * P], KT[:, h, kt * 128:kt * 128 + nk], qT[:, h, 0:P], [("KT", kt), "qT"], [Sn], acc=(j > 0))

                    def st_E(li, kt, nk, diag, half):
                        Sn, Ln_, en, spn, spbn, wn, rn = names(half)
                        act(e_t[half][0:nk, 0:W], psS[half][0:nk, 0:W], AF.Exp, [Sn], [en])
                        act(sp_t[half][0:nk, 0:W], e_t[half][0:nk, 0:W], AF.Ln, [en], [spn], bias=1.0)

                    def st_C(li, kt, nk, diag, half):
                        Sn, Ln_, en, spn, spbn, wn, rn = names(half)
                        sp_ = sp_t[half][0:nk, 0:W]; spb_ = spb_t[half][0:nk, 0:W]
                        if diag:
                            tt("dve", spb_.rearrange("k (j q) -> k j q", j=4), sp_.rearrange("k (j q) -> k j q", j=4),
                               mask01[0:nk, 0:P].unsqueeze(1).to_broadcast([nk, 4, P]), ALU.mult, [spn, "cst_b"], [spbn])
                        else:
                            cp("dve", spb_, sp_, [spn], [spbn])

                    def st_L(li, kt, nk, diag, half):
                        Sn, Ln_, en, spn, spbn, wn, rn = names(half)
                        first = li == 0
                        Lb = psL[half][0:nk, 0:W]; spb_ = spb_t[half][0:nk, 0:W]
                        mm(Lb, negtri[0:nk, 0:nk], spb_, [spbn, "cst_b"], [Ln_], start=True, stop=False)
                        if not first:
                            mm(Lb, negones[:, 0:nk], Racc[half][:, 0:W], [rn, "cst_b"], [Ln_], start=False, stop=False, acc=True)
                        for j in range(4):
                            h = 4 * half + j
                            mm(psL[half][0:nk, j * P:(j + 1) * P], KT[:, h, kt * 128:kt * 128 + nk], qT[:, h, 0:P], [("KT", kt), "qT"], [Ln_],
                               start=False, stop=(j == 3), acc=True)

                    def st_RT(li, kt, nk, diag, half):
                        Sn, Ln_, en, spn, spbn, wn, rn = names(half)
                        last = li == nkt_ - 1
                        spb_ = spb_t[half][0:nk, 0:W]
                        if not last:
                            tt("dve", Racc[half][0:nk, 0:W], Racc[half][0:nk, 0:W], spb_, ALU.add, [rn, spbn], [rn])
                        tt("dve", tbuf(half)[0:nk, 0:W], psL[half][0:nk, 0:W], sp_t[half][0:nk, 0:W], ALU.subtract, [Ln_, spn], [("segb", half)])

                    def st_W(li, kt, nk, diag, half):
                        Sn, Ln_, en, spn, spbn, wn, rn = names(half)
                        w_ = w_t[half][0:nk, 0:W]
                        act(w_, tbuf(half)[0:nk, 0:W], AF.Exp, [("segb", half)], [wn])
                        if diag:
                            tt("pool", w_.rearrange("k (j q) -> k j q", j=4), w_.rearrange("k (j q) -> k j q", j=4),
                               mask01[0:nk, 0:P].unsqueeze(1).to_broadcast([nk, 4, P]), ALU.mult, [wn, "cst_b"], [wn])

                    def st_PV(li, kt, nk, diag, half):
                        wn = names(half)[5]
                        for j in range(4):
                            h = 4 * half + j
                            mm(psO[0:P, h * 64:(h + 1) * 64], w_t[half][0:nk, j * P:(j + 1) * P], V[0:nk, kt, h * 64:(h + 1) * 64],
                               [wn, ("V", kt)], ["psO"], start=False, stop=False, acc=True, skip_group_check=True)

                    U_ = len(units)

                    def run_st(fn, idx):
                        if 0 <= idx < U_:
                            fn(*units[idx])

                    for it in range(-2, U_ + 1):
                        run_st(st_S, it + 2)
                        run_st(st_L, it)
                        run_st(st_PV, it - 1)
                        run_st(st_RT, it)
                        run_st(st_E, it + 2)
                        run_st(st_C, it + 2)
                        run_st(st_W, it)
                    mm(psO[0:P, :], zb[:, 0:P], zb[:, :], ["zb"], ["psO"], start=False, stop=True, acc=True)
                    act(xn[0:P, 0:512], psO[0:P, :], AF.Square, ["psO"], ["xn", "col"], accum_out=col[0:P, 8:9])
                    act(col[0:P, 9:10], col[0:P, 8:9], AF.Ln, ["col"], ["col"], scale=1.0 / DATT, bias=EPS)
                    act(col[0:P, 10:11], col[0:P, 9:10], AF.Exp, ["col"], ["col"], scale=-0.5)
                    stt(attn_n[0:P, :], psO[0:P, :], col[0:P, 10:11], g_attn[0:P, :], ALU.mult, ALU.mult, ["psO", "col", "gbc"], ["attn_n"])
                    for c in range(4):
                        tr(psT[:, c * 128:c * 128 + P], attn_n[0:P, c * 128:(c + 1) * 128], ident_b[0:P, 0:P], ["attn_n", "cst_b"], ["psT"], acc=(c > 0))
                    cp("act", attn_nT[:, :, 0:P], psT[:, 0:512].rearrange("p (c t) -> p c t", c=4)[:, :, 0:P], ["psT"], ["attn_nT"])

                    chk(6)
                    mm(psL[1][0:P, 0:8], triincl[0:P, 0:P], dA[0:P, :], ["dA", "cst_f"], ["psL1"])
                    cp("act", cum[0:P, :], psL[1][0:P, 0:8], ["psL1"], ["cum"])
                    tt("dve", segb[0:P, :, 0:P], dA[0:P, :].unsqueeze(2).to_broadcast([P, 8, P]),
                       triincl[0:P, 0:P].unsqueeze(1).to_broadcast([P, 8, P]), ALU.mult, ["dA", "cst_f"], [("segb", 0), ("segb", 1)])
                    for hf in range(2):
                        mm(psS[hf][:, 0:4 * P], ones_f[0:P, :], segb[0:P, 4 * hf:4 * hf + 4, 0:P], [("segb", 0), ("segb", 1), "cst_f"], ["psS%d" % hf])
                    for c in range(4):
                        tr(psA[0:P, c * 128:(c + 1) * 128], xcT[:, c, 0:P], ident_f, [("xcT", c), "cst_f"], ["psA"], acc=(c > 0))
                    for j in range(2):
                        tr(psB[0:P, j * 64:(j + 1) * 64], xcT[0:64, 4 + j, 0:P], ident_f[0:64, 0:64], [("xcT", 4 + j), "cst_f"], ["psB"], acc=(j > 0))
                    cp("act", x_tok[0:P, :], psA[0:P, :], ["psA"], ["x_tok"])
                    cp("act", B_tok[0:P, :], psB[0:P, 0:128], ["psB"], ["B_tok"])
                    tt("dve", xdt[0:P, :].rearrange("p (h d) -> p h d", h=8), x_tok[0:P, :].rearrange("p (h d) -> p h d", h=8),
                       dtt[0:P, :].unsqueeze(2).to_broadcast([P, 8, 64]), ALU.mult, ["x_tok", "dtt"], ["xdt"])
                    tt("pool", x_tok[0:P, :], x_tok[0:P, :], d_bc[0:P, :], ALU.mult, ["x_tok", "d_bc"], ["x_tok"])
                    for g in range(2):
                        mm(psB[0:P, 256 + g * 128:256 + g * 128 + P], BCTb[:, g, 0:P], BCTb[:, 2 + g, 0:P], ["BCTb"], ["psB"], acc=True)
                    for hf in range(2):
                        pn = "psS%d" % hf
                        ab = psS[hf][:, 0:4 * P].rearrange("p (r t) -> p r t", r=4)
                        hs = slice(4 * hf, 4 * hf + 4)
                        tt("dve", te[0:P, hs], ab[0:P, :, P - 1], cum[0:P, hs], ALU.subtract, [pn, "cum"], [("te", hf)])
                        act(E_bc[:, hs, 0:P], ab[0:64, :, :], AF.Exp, [pn], [("E_bc", hf)])
                        for r_ in range(4):
                            stt(segb[0:P, 4 * hf + r_, 0:P], ab[0:P, r_, :], cum[0:P, 4 * hf + r_:4 * hf + r_ + 1], negmask[0:P, 0:P],
                                ALU.subtract, ALU.add, [pn, "cum", "cst_f"], [("segb", hf)])
                        act(segb[0:P, hs, 0:P], segb[0:P, hs, 0:P], AF.Exp, [("segb", hf)], [("segb", hf)])
                        tt("dve", MT[0:P, hs, 0:P], segb[0:P, hs, 0:P],
                           psB[0:P, 256 + hf * 128:256 + hf * 128 + P].unsqueeze(1).to_broadcast([P, 4, P]), ALU.mult,
                           [("segb", hf), "psB"], [("MT", hf)])
                        tt("dve", CsT[:, hs, 0:P], E_bc[:, hs, 0:P], BCTb[:, 2 + hf, 0:P].unsqueeze(1).to_broadcast([64, 4, P]), ALU.mult,
                           [("E_bc", hf), "BCTb"], [("CsT", hf)])
                    act(te[0:P, :], te[0:P, :], AF.Exp, [("te", 0), ("te", 1)], [("te", 0), ("te", 1)])
                    tt("dve", xw[0:P, :].rearrange("p (h d) -> p h d", h=8), xdt[0:P, :].rearrange("p (h d) -> p h d", h=8),
                       te[0:P, :].unsqueeze(2).to_broadcast([P, 8, 64]), ALU.mult, ["xdt", ("te", 0), ("te", 1)], ["xw"])
                    for h in range(8):
                        hf = h // 4
                        mm(psL[0][0:P, h * 64:(h + 1) * 64], MT[0:P, h, 0:P], xdt[0:P, h * 64:(h + 1) * 64], [("MT", hf), "xdt"], ["psL0"],
                           start=True, stop=False, acc=(h > 0))
                        mm(psL[0][0:P, h * 64:(h + 1) * 64], CsT[:, h, 0:P], hTb[:, h, :], [("CsT", hf), "hTb"], ["psL0"],
                           start=False, stop=True, acc=True)
                    for g in range(2):
                        mm(psL[1][0:64, g * 256:(g + 1) * 256], B_tok[0:P, g * 64:(g + 1) * 64], xw[0:P, g * 256:(g + 1) * 256], ["B_tok", "xw"], ["psL1"],
                           acc=(g > 0))
                    tt("dve", htmp[:], hT[:], E_bc[:, :, P - 1].unsqueeze(2).to_broadcast([64, 8, 64]), ALU.mult,
                       ["hT", ("E_bc", 0), ("E_bc", 1)], ["htmp"])
                    tt("dve", hT[:].rearrange("n h p -> n (h p)"), htmp[:].rearrange("n h p -> n (h p)"), psL[1][0:64, :], ALU.add, ["htmp", "psL1"], ["hT"])
                    cp("pool", hTb[:], hT[:], ["hT"], ["hTb"])
                    tt("dve", yf[0:P, :], psL[0][0:P, :], x_tok[0:P, :], ALU.add, ["psL0", "x_tok"], ["yf"])
                    tt("dve", yf[0:P, :], yf[0:P, :], sz[0:P, :], ALU.mult, ["yf", "sz"], ["yf"])
                    act(xn[0:P, 0:512], yf[0:P, :], AF.Square, ["yf"], ["xn", "col"], accum_out=col[0:P, 3:4])
                    act(col[0:P, 4:5], col[0:P, 3:4], AF.Ln, ["col"], ["col"], scale=1.0 / DSSM, bias=EPS)
                    act(col[0:P, 5:6], col[0:P, 4:5], AF.Exp, ["col"], ["col"], scale=-0.5)
                    stt(yn[0:P, :], yf[0:P, :], col[0:P, 5:6], g_ssm[0:P, :], ALU.mult, ALU.mult, ["yf", "col", "gbc"], ["yn"])
                    for c in range(4):
                        tr(psT[:, c * 128:c * 128 + P], yn[0:P, c * 128:(c + 1) * 128], ident_b[0:P, 0:P], ["yn", "cst_b"], ["psT"], acc=(c > 0))
                    cp("act", ynT[:, :, 0:P], psT[:, 0:512].rearrange("p (c t) -> p c t", c=4)[:, :, 0:P], ["psT"], ["ynT"])

                    chk(7)
                    to = tmpo[0]; ton = "tmpo0"
                    for hf in range(2):
                        bank = psA if hf == 0 else psB
                        bname = "psA" if hf == 0 else "psB"
                        for c in range(4):
                            mm(bank[0:P, :], attn_nT[:, c, 0:P], wo[:, c, hf * 512:(hf + 1) * 512], ["attn_nT", "wo"], [bname],
                               start=(c == 0), stop=False, acc=(c > 0))
                        for c in range(4):
                            mm(bank[0:P, :], ynT[:, c, 0:P], wo[:, 4 + c, hf * 512:(hf + 1) * 512], ["ynT", "wo"], [bname],
                               start=False, stop=(c == 3), acc=True)
                        act(xn[0:P, 0:512], bank[0:P, :], AF.Square, [bname], ["xn", "col"], accum_out=col[0:P, 6 + hf:7 + hf])
                    tt("dve", col[0:P, 6:7], col[0:P, 6:7], col[0:P, 7:8], ALU.add, ["col"], ["col"])
                    act(col[0:P, 6:7], col[0:P, 6:7], AF.Ln, ["col"], ["col"], scale=1.0 / D, bias=EPS)
                    act(col[0:P, 7:8], col[0:P, 6:7], AF.Exp, ["col"], ["col"], scale=-0.5)
                    for hf in range(2):
                        bank = psA if hf == 0 else psB
                        bname = "psA" if hf == 0 else "psB"
                        stt(to[0:P, hf * 512:(hf + 1) * 512], bank[0:P, :], col[0:P, 7:8], g_post[0:P, hf * 512:(hf + 1) * 512], ALU.mult, ALU.mult,
                            [bname, "col", "gbc"], [ton])
                    tt("dve", to[0:P, 0:512], to[0:P, 0:512], xb[0:P, 0:512], ALU.add, [ton, xbn], [ton])
                    tt("pool", to[0:P, 512:1024], to[0:P, 512:1024], xb[0:P, 512:1024], ALU.add, [ton, xbn], [ton])
                    dma("sp", S["y"][r0:r0 + P, :], to[0:P, :], [ton], [("ydram", sq_i, ti)], ton + "_o")
                    gti += 1
                    chk(71)

                chk(72)
                dma("sp", S["ss"], hT[:].rearrange("n h p -> n (h p)"), ["hT"], [], "hT_o")
                dma("sp", S["sc"], xbcT[:, :, 0:3], ["xbcT"], [], "xbcT_o")
                chk(73)

            chk(8)
            _st = _STOPPED[0]; _STOPPED[0] = False
            pr.barrier()
            pr.emit()
            _STOPPED[0] = _st

        p2 = ExitStack()
        with p2:
            def sb2(name, shape, dt=F32):
                return p2.enter_context(nc.sbuf_tensor(name, list(shape), dt))

            gbc2 = sb2("gbc2", [128, 2048])
            g_fpre = gbc2[:, 0:1024]; g_fpost = gbc2[:, 1024:2048]
            dma("sp", gbc2[:].unsqueeze(1), gvec[:, 3072:5120].partition_broadcast(128), (), ["gbc"], "gbc2")
            wup = sb2("wup", [128, 8, 2 * DFF], BF16)
            wdn = sb2("wdn", [128, NFF, D], BF16)
            for gi_, j0_ in enumerate(range(0, NFF, 4)):
                nj_ = min(4, NFF - j0_)
                for q4 in range(2):
                    c0_ = q4 * DFF + j0_ * 128
                    dma("pool", wup[:, :, c0_:c0_ + nj_ * 128], w_up[:, c0_:c0_ + nj_ * 128].rearrange("(c p) n -> p c n", p=128), (),
                        [("wup", gi_)], "wup%d" % gi_, max_dma_last_dim=4096)
            for j in range(NFF):
                dma("pool", wdn[:, j, :], w_dn[j * 128:(j + 1) * 128, :], (), ["wdn"], "wdn", max_dma_last_dim=4096)
            hblk = [sb2("hblk0", [128, D]), sb2("hblk1", [128, D]), sb2("hblk2", [128, D])]
            junk2 = sb2("junk2", [128, D], BF16)
            hn = sb2("hn", [128, D], BF16)
            hnT = [sb2("hnT0", [128, 8, 128], BF16), sb2("hnT1", [128, 8, 128], BF16)]
            col2 = sb2("col2", [128, 8]); colp = sb2("colp", [128, 4])
            fhist = sb2("fhist", [128, NFF, 2])
            gs = [sb2("gs0", [128, 4, 130]), sb2("gs1", [128, 4, 130])]
            acc_t = [sb2("acc0", [128, 4, 128]), sb2("acc1", [128, 4, 128])]
            actT = [sb2("actT0", [128, NFF, 128], BF16), sb2("actT1", [128, NFF, 128], BF16)]
            yo = [sb2("yo0", [128, D]), sb2("yo1", [128, D])]

            tiles2 = []
            for sq_i, S in enumerate(seqs):
                P_ = min(128, S["n"])
                for ti in range(S["n"] // P_):
                    tiles2.append((sq_i, S, ti, P_))
            NT2 = len(tiles2)

            def emit_hload(g):
                if g >= NT2:
                    return
                sq_i, S, ti, P = tiles2[g]
                r0 = S["row0"] + ti * P
                hbn = "hblk%d" % (g % 3)
                dma("sp", hblk[g % 3][0:P, :], S["y"][r0:r0 + P, :], [("ydram", sq_i, ti)], [hbn], hbn)

            def prologue(g):
                if g >= NT2:
                    return
                sq_i, S, ti, P = tiles2[g]
                hb = hblk[g % 3]; hbn = "hblk%d" % (g % 3)
                hT_ = hnT[g % 2]; hTn = "hnT%d" % (g % 2)
                act(junk2[0:P, :], hb[0:P, :], AF.Square, [hbn], ["junk2", "colp"], accum_out=colp[0:P, 0:1])
                act(colp[0:P, 1:2], colp[0:P, 0:1], AF.Ln, ["colp"], ["colp"], scale=1.0 / D, bias=EPS)
                act(colp[0:P, 2:3], colp[0:P, 1:2], AF.Exp, ["colp"], ["colp"], scale=-0.5)
                stt(hn[0:P, :], hb[0:P, :], colp[0:P, 2:3], g_fpre[0:P, :], ALU.mult, ALU.mult, [hbn, "colp", "gbc"], ["hn"])
                for dc in range(8):
                    tr(psT[:, dc * 128:dc * 128 + P], hn[0:P, dc * 128:(dc + 1) * 128], ident_b[0:P, 0:P], ["hn", "cst_b"], ["psT"], acc=(dc > 0))
                cp("act", hT_[:, :, 0:P], psT[:, :].rearrange("p (c t) -> p c t", c=8)[:, :, 0:P], ["psT"], [hTn])

            def up_proj(g):
                if g >= NT2:
                    return
                sq_i, S, ti, P = tiles2[g]
                nt = S["n"] // P
                hT_ = hnT[g % 2]; hTn = "hnT%d" % (g % 2)
                aT = actT[g % 2]
                if ti == 0:
                    if S["idx"] is None:
                        mset("dve", fhist[:], 0.0, ["fhist"])
                    else:
                        dma("sp", fhist[:], sfc[S["idx"]], (), ["fhist"], "fhist")
                groups = [(j0, min(4, NFF - j0)) for j0 in range(0, NFF, 4)]

                def stPA(gi):
                    j0, nj = groups[gi]
                    i2 = gi % 2
                    gb = psS[i2]; gbn = "psS%d" % i2
                    ub = psL[i2]; ubn = "psL%d" % i2
                    for jj in range(nj):
                        j = j0 + jj
                        for dc in range(8):
                            mm(gb[:, jj * 128:jj * 128 + P], wup[:, dc, j * 128:(j + 1) * 128], hT_[:, dc, 0:P], [("wup", gi), hTn], [gbn],
                               start=(dc == 0), stop=(dc == 7), acc=(dc > 0 or jj > 0))
                    for jj in range(nj):
                        j = j0 + jj
                        for dc in range(8):
                            mm(ub[:, jj * 128:jj * 128 + P], wup[:, dc, DFF + j * 128:DFF + (j + 1) * 128], hT_[:, dc, 0:P], [("wup", gi), hTn], [ubn],
                               start=(dc == 0), stop=(dc == 7), acc=(dc > 0 or jj > 0))
                    g_ = gs[i2]; gn = "gs%d" % i2
                    a_ = acc_t[i2]; an = "acc%d" % i2
                    gview = gb[:, :].rearrange("p (j t) -> p j t", j=4)[:, 0:nj, 0:P]
                    cp("dve", g_[:, 0:nj, 0:2], fhist[:, j0:j0 + nj, :], ["fhist"], [gn])
                    cp("act", g_[:, 0:nj, 2:2 + P], gview, [gbn], [gn])
                    cp("dve", fhist[:, j0:j0 + nj, :], g_[:, 0:nj, P:P + 2], [gn], ["fhist"])
                    for jj in range(nj):
                        j = j0 + jj
                        act(a_[:, jj, 0:P], gb[:, jj * 128:jj * 128 + P], AF.Identity, [gbn, "wc_f"], [(an, jj)],
                            scale=wc_f[:, j, 2:3], bias=wc_f[:, j, 3:4])
                    for tap in (1, 0):
                        for jj in range(nj):
                            j = j0 + jj
                            stt(a_[:, jj, 0:P], g_[:, jj, tap:tap + P], wc_f[:, j, tap:tap + 1], a_[:, jj, 0:P], ALU.mult, ALU.add,
                                [gn, "wc_f", (an, jj)], [(an, jj)])

                def stB(gi):
                    j0, nj = groups[gi]
                    i2 = gi % 2
                    ub = psL[i2]; ubn = "psL%d" % i2
                    a_ = acc_t[i2]; an = "acc%d" % i2
                    uview = ub[:, :].rearrange("p (j t) -> p j t", j=4)[:, 0:nj, 0:P]
                    a_all = [(an, jj) for jj in range(4)]
                    act(a_[:, 0:nj, 0:P], a_[:, 0:nj, 0:P], AF.Gelu_apprx_tanh, a_all, a_all)
                    tt("dve", aT[:, j0:j0 + nj, 0:P], a_[:, 0:nj, 0:P], uview, ALU.mult, a_all + [ubn], [("actT", g % 2, gi)])

                for gi in range(len(groups) + 1):
                    if gi < len(groups):
                        stPA(gi)
                    if gi >= 1:
                        stB(gi - 1)
                if ti == nt - 1:
                    dma("sp", S["fc"], fhist[:], ["fhist"], [], "fhist_o")

            def down_proj(g):
                sq_i, S, ti, P = tiles2[g]
                r0 = S["row0"] + ti * P
                hb = hblk[g % 3]; hbn = "hblk%d" % (g % 3)
                aT = actT[g % 2]
                yo_ = yo[g % 2]; yon = "yo%d" % (g % 2)
                for hf in range(2):
                    bank = psA if hf == 0 else psB
                    bname = "psA" if hf == 0 else "psB"
                    for j in range(NFF):
                        mm(bank[0:P, :], aT[:, j, 0:P], wdn[:, j, hf * 512:(hf + 1) * 512], [("actT", g % 2, j // 4), "wdn"], [bname],
                           start=(j == 0), stop=(j == NFF - 1), acc=(j > 0))
                    act(junk2[0:P, 0:512], bank[0:P, :], AF.Square, [bname], ["junk2", "col2"], accum_out=col2[0:P, 3 + hf:4 + hf])
                tt("dve", col2[0:P, 3:4], col2[0:P, 3:4], col2[0:P, 4:5], ALU.add, ["col2"], ["col2"])
                act(col2[0:P, 5:6], col2[0:P, 3:4], AF.Ln, ["col2"], ["col2"], scale=1.0 / D, bias=EPS)
                act(col2[0:P, 6:7], col2[0:P, 5:6], AF.Exp, ["col2"], ["col2"], scale=-0.5)
                for hf in range(2):
                    bank = psA if hf == 0 else psB
                    bname = "psA" if hf == 0 else "psB"
                    stt(yo_[0:P, hf * 512:(hf + 1) * 512], bank[0:P, :], col2[0:P, 6:7], g_fpost[0:P, hf * 512:(hf + 1) * 512], ALU.mult, ALU.mult,
                        [bname, "col2", "gbc"], [(yon, hf)])
                tt("dve", yo_[0:P, 0:512], yo_[0:P, 0:512], hb[0:P, 0:512], ALU.add, [(yon, 0), hbn], [(yon, 0)])
                tt("pool", yo_[0:P, 512:1024], yo_[0:P, 512:1024], hb[0:P, 512:1024], ALU.add, [(yon, 1), hbn], [(yon, 1)])
                dma("sp", S["y"][r0:r0 + P, :], yo_[0:P, :], [(yon, 0), (yon, 1)], [("ydram2", sq_i, ti)], yon + "_o")
                emit_hload(g + 3)

            emit_hload(0)
            emit_hload(1)
            emit_hload(2)
            prologue(0)
            prologue(1)
            up_proj(0)
            for g in range(NT2):
                up_proj(g + 1)
                prologue(g + 2)
                down_proj(g)

            _STOPPED[0] = False
            pr.barrier(final=True)
            pr.emit()
    return nc


def _consts():
    j = np.arange(128)[:, None]; s = np.arange(128)[None, :]
    ident = (j == s).astype(np.float32)
    triincl = (j <= s).astype(np.float32)
    ones = np.ones((128, 128), np.float32)
    negmask = np.where(j <= s, 0.0, -1e5).astype(np.float32)
    pad = np.zeros((128, 128), np.float32)
    cst = np.concatenate([ident, triincl, ones, negmask, pad], axis=1)
    negtri = -(j > s).astype(np.float32)
    negones = -ones
    mask01 = (j < s).astype(np.float32)
    cstb = np.concatenate([ident, negtri, negones, mask01], axis=1)
    return np.ascontiguousarray(cst), np.ascontiguousarray(cstb)


def _chan_layout(a):
    lead = a.shape[:-1]
    out = np.zeros((128, 8) + lead, np.float32)
    for c in range(4):
        out[:, c] = np.moveaxis(a[..., c * 128:(c + 1) * 128], -1, 0)
    for c in range(4):
        out[0:64, 4 + c] = np.moveaxis(a[..., 512 + c * 64:512 + (c + 1) * 64], -1, 0)
    return out


def _chan_unlayout(d):
    k = d.shape[2]
    out = np.zeros((k, 768), np.float32)
    for c in range(4):
        out[:, c * 128:(c + 1) * 128] = d[:, c, :].T
    for c in range(4):
        out[:, 512 + c * 64:512 + (c + 1) * 64] = d[0:64, 4 + c, :].T
    return out


def make_in_maps(inp, n_cores, NP, PAST, NS, NSEQ):
    cst, cstb = _consts()
    f = lambda a: np.ascontiguousarray(np.asarray(a, dtype=np.float32))
    gvec = np.concatenate([inp["g_mix_pre"][0], inp["g_mix_post"][0], inp["g_ssm_out"][0], inp["g_attn_out"][0], inp["g_ffn_pre"][0], inp["g_ffn_post"][0]])[None, :]
    hvec = np.concatenate([inp["dt_bias"][0], inp["a_log"][0], inp["d_skip"][0]])[None, :]
    wcs = _chan_layout(np.concatenate([inp["ssm_conv_w"][0], inp["ssm_conv_b"][0][None, :]], axis=0))
    wcf_src = np.concatenate([inp["ffn_conv_w"][0], inp["ffn_conv_b"][0][None, :]], axis=0)
    wcf = wcf_src.reshape(4, NFF, 128).transpose(2, 1, 0)
    shared = dict(w_in=f(inp["w_in"][0]), w_out=f(inp["w_out"][0]), w_up=f(inp["w_up"][0]), w_dn=f(inp["w_down"][0]),
                  gvec=f(gvec), hvec=f(hvec), wcs=f(wcs), wcf=f(wcf), cst=cst, cstb=cstb)
    maps = []
    for c in range(n_cores):
        bs = list(range(c * NSEQ, (c + 1) * NSEQ))
        m = dict(shared)
        m["xp"] = f(inp["x_prompt"][c])
        m["xs"] = f(inp["x_sample"][bs].reshape(NSEQ * NS, D))
        m["ck"] = f(inp["cache_k"][0, bs].reshape(NSEQ, PAST, DATT))
        m["cv"] = f(inp["cache_v"][0, bs].reshape(NSEQ, PAST, DATT))
        m["sst"] = f(inp["state_ssm"][0, bs].transpose(0, 3, 1, 2).reshape(NSEQ, NST, DSSM))
        m["scv"] = f(np.stack([_chan_layout(inp["state_ssm_conv"][0, b]) for b in bs]))
        m["sfc"] = f(np.stack([inp["state_ffn_conv"][0, b].reshape(2, NFF, 128).transpose(2, 1, 0) for b in bs]))
        maps.append(m)
    return maps


def gather(results, n_cores, NP, PAST, NS, NSEQ):
    B = n_cores; BS = n_cores * NSEQ
    y_p = np.stack([r["yp"] for r in results])
    y_s = np.concatenate([r["ys"].reshape(NSEQ, NS, D) for r in results])
    k_p = np.stack([r["kTp"].transpose(2, 0, 1) for r in results])[None]
    v_p = np.stack([r["vp"].reshape(NP, NH, HD) for r in results])[None]
    ssm_p = np.stack([r["ssp"].reshape(NST, NH, HD).transpose(1, 2, 0) for r in results])[None]
    sc_p = np.stack([_chan_unlayout(r["scp"]) for r in results])[None]
    fc_p = np.stack([r["fcp"].transpose(2, 1, 0).reshape(2, DFF) for r in results])[None]
    k_s = np.concatenate([r["kTs"].transpose(0, 3, 1, 2) for r in results])[None]
    v_s = np.concatenate([r["vs"].reshape(NSEQ, NS, NH, HD) for r in results])[None]
    ssm_s = np.concatenate([r["sss"].reshape(NSEQ, NST, NH, HD).transpose(0, 2, 3, 1) for r in results])[None]
    sc_s = np.concatenate([np.stack([_chan_unlayout(r["scs"][b]) for b in range(NSEQ)]) for r in results])[None]
    fc_s = np.concatenate([np.stack([r["fcs"][b].transpose(2, 1, 0).reshape(2, DFF) for b in range(NSEQ)]) for r in results])[None]
    outs = (y_p, y_s, k_p, v_p, ssm_p, sc_p, fc_p, k_s, v_s, ssm_s, sc_s, fc_s)
    return tuple(np.ascontiguousarray(o, dtype=np.float32) for o in outs)


def kernel(**inputs):
    inp = {k: np.asarray(v) for k, v in inputs.items()}
    n_cores = 8
    NP = inp["x_prompt"].shape[1]; PAST = inp["cache_k"].shape[2]; NS = inp["x_sample"].shape[1]
    NSEQ = inp["x_sample"].shape[0] // n_cores
    nc = build(NP, PAST, NS, NSEQ)
    maps = make_in_maps(inp, n_cores, NP, PAST, NS, NSEQ)
    res = run_bass_kernel_spmd(nc, maps, core_ids=list(range(n_cores)))
    return gather(res.results, n_cores, NP, PAST, NS, NSEQ)
```
